# Optimizing a Trainium2 kernel written in Bass

```python
import jax, jax.numpy as jnp
from jax import lax
import numpy as np

D_MODEL = 1024
BATCH = 2
SEQ = 8192
DEPTH = 2

N_A_LAYERS = DEPTH // 2
N_B_LAYERS = DEPTH - N_A_LAYERS
HEAD_DIM = 64
SB_HEADS = D_MODEL // HEAD_DIM
SWA_Q_HEADS = D_MODEL // HEAD_DIM
SWA_KV_HEADS = 4
SWA_GROUP = SWA_Q_HEADS // SWA_KV_HEADS
WINDOW = 128
Q_BLOCK = 128
D_FF = 2816
ROPE_THETA = 10000.0
RMS_EPS = 1e-6
FFN_RES_SCALE = 0.5

kernel_name = "yoco_stickbreak_swa_sink_macaron"


def rms_norm(x, g):
    xf = x.astype(jnp.float32)
    y = xf * lax.rsqrt(jnp.mean(xf * xf, axis=-1, keepdims=True) + RMS_EPS)
    return (y * g.astype(jnp.float32)).astype(x.dtype)


def swiglu(x, w_in, w_out):
    gate, up = jnp.split(x @ w_in, 2, axis=-1)
    return (jax.nn.silu(gate) * up) @ w_out


def rotary(x, pos):
    half = HEAD_DIM // 2
    inv_freq = ROPE_THETA ** (-jnp.arange(half, dtype=jnp.float32) / half)
    ang = pos.astype(jnp.float32)[:, None] * inv_freq[None, :]
    cos = jnp.cos(ang)[None, :, None, :]
    sin = jnp.sin(ang)[None, :, None, :]
    xf = x.astype(jnp.float32)
    x1, x2 = xf[..., :half], xf[..., half:]
    return jnp.concatenate([x1 * cos - x2 * sin, x2 * cos + x1 * sin], axis=-1).astype(x.dtype)


def stick_breaking_attention(q, k, v):
    B, S, H, Dh = q.shape
    nb = S // Q_BLOCK
    scale = Dh ** -0.5
    key_pos = jnp.arange(S)
    qb = q.reshape(B, nb, Q_BLOCK, H, Dh).transpose(1, 0, 2, 3, 4)

    def block(args):
        qi, i = args
        z = jnp.einsum('bqhd,bshd->bhqs', qi, k).astype(jnp.float32) * scale
        q_pos = i * Q_BLOCK + jnp.arange(Q_BLOCK)
        strict = key_pos[None, :] < q_pos[:, None]
        log_beta = jax.nn.log_sigmoid(z)
        log_1m_beta = jnp.where(strict, log_beta - z, 0.0)
        suffix = lax.cumsum(log_1m_beta, axis=3, reverse=True) - log_1m_beta
        w = jnp.where(strict, jnp.exp(log_beta + suffix), 0.0)
        return jnp.einsum('bhqs,bshd->bqhd', w.astype(v.dtype), v)

    out = lax.map(block, (qb, jnp.arange(nb)))
    return out.transpose(1, 0, 2, 3, 4).reshape(B, S, H, Dh)


def sliding_window_sink_attention(q, k, v, sinks):
    B, S, Hq, Dh = q.shape
    nb = S // WINDOW
    qb = q.reshape(B, nb, WINDOW, SWA_KV_HEADS, SWA_GROUP, Dh)

    def band(t):
        tb = t.reshape(B, nb, WINDOW, SWA_KV_HEADS, Dh)
        prev = jnp.pad(tb[:, :-1], ((0, 0), (1, 0), (0, 0), (0, 0), (0, 0)))
        return jnp.concatenate([prev, tb], axis=2)

    kb, vb = band(k), band(v)
    s = jnp.einsum('bnqhgd,bnkhd->bnhgqk', qb, kb).astype(jnp.float32) * (Dh ** -0.5)
    qi = jnp.arange(WINDOW)[:, None]
    ki = jnp.arange(2 * WINDOW)[None, :]
    diff = qi + WINDOW - ki
    in_window = (diff >= 0) & (diff < WINDOW)
    blk = jnp.arange(nb)[:, None, None]
    valid = in_window[None] & ((blk > 0) | (ki[None] >= WINDOW))
    s = jnp.where(valid[None, :, None, None], s, -jnp.inf)
    sink = jnp.broadcast_to(
        sinks.astype(jnp.float32).reshape(SWA_KV_HEADS, SWA_GROUP)[None, None, :, :, None, None],
        s.shape[:-1] + (1,))
    p = jax.nn.softmax(jnp.concatenate([s, sink], axis=-1), axis=-1)[..., :-1]
    out = jnp.einsum('bnhgqk,bnkhd->bnqhgd', p.astype(v.dtype), vb)
    return out.reshape(B, S, Hq, Dh)


def setup_inputs(seed: int = 0) -> dict:
    key = jax.random.key(seed)
    ks = jax.random.split(key, 20)
    f32 = jnp.float32

    def w(k, shape, fan_in):
        return jax.random.normal(k, shape, f32) * (fan_in ** -0.5)

    def gain(k, shape):
        return 1.0 + 0.02 * jax.random.normal(k, shape, f32)

    return {
        "x": jax.random.normal(ks[0], (BATCH, SEQ, D_MODEL), f32),
        "ffn1_norm": gain(ks[1], (DEPTH, D_MODEL)),
        "ffn1_w_in": w(ks[2], (DEPTH, D_MODEL, 2 * D_FF), D_MODEL),
        "ffn1_w_out": w(ks[3], (DEPTH, D_FF, D_MODEL), D_FF),
        "mix_norm": gain(ks[4], (DEPTH, D_MODEL)),
        "ffn2_norm": gain(ks[5], (DEPTH, D_MODEL)),
        "ffn2_w_in": w(ks[6], (DEPTH, D_MODEL, 2 * D_FF), D_MODEL),
        "ffn2_w_out": w(ks[7], (DEPTH, D_FF, D_MODEL), D_FF),
        "sb_w_qkv": w(ks[8], (N_A_LAYERS, D_MODEL, 3 * SB_HEADS * HEAD_DIM), D_MODEL),
        "sb_w_o": w(ks[9], (N_A_LAYERS, SB_HEADS * HEAD_DIM, D_MODEL), SB_HEADS * HEAD_DIM),
        "kv_norm": gain(ks[10], (D_MODEL,)),
        "kv_w": w(ks[11], (D_MODEL, 2 * SWA_KV_HEADS * HEAD_DIM), D_MODEL),
        "swa_w_q": w(ks[12], (N_B_LAYERS, D_MODEL, SWA_Q_HEADS * HEAD_DIM), D_MODEL),
        "swa_sinks": 0.5 * jax.random.normal(ks[13], (N_B_LAYERS, SWA_Q_HEADS), f32),
        "swa_w_o": w(ks[14], (N_B_LAYERS, SWA_Q_HEADS * HEAD_DIM, D_MODEL), SWA_Q_HEADS * HEAD_DIM),
        "final_norm": gain(ks[15], (D_MODEL,)),
    }


def reference(x, ffn1_norm, ffn1_w_in, ffn1_w_out, mix_norm, ffn2_norm, ffn2_w_in, ffn2_w_out,
              sb_w_qkv, sb_w_o, kv_norm, kv_w, swa_w_q, swa_sinks, swa_w_o, final_norm):
    B, S, D = x.shape
    pos = jnp.arange(S)
    h = x
    k_shared = None
    v_shared = None
    for layer in range(DEPTH):
        h = h + FFN_RES_SCALE * swiglu(rms_norm(h, ffn1_norm[layer]), ffn1_w_in[layer], ffn1_w_out[layer])
        hn = rms_norm(h, mix_norm[layer])
        if layer < N_A_LAYERS:
            qkv = (hn @ sb_w_qkv[layer]).reshape(B, S, 3, SB_HEADS, HEAD_DIM)
            o = stick_breaking_attention(qkv[:, :, 0], qkv[:, :, 1], qkv[:, :, 2])
            h = h + o.reshape(B, S, SB_HEADS * HEAD_DIM) @ sb_w_o[layer]
        else:
            j = layer - N_A_LAYERS
            q = rotary((hn @ swa_w_q[j]).reshape(B, S, SWA_Q_HEADS, HEAD_DIM), pos)
            o = sliding_window_sink_attention(q, k_shared, v_shared, swa_sinks[j])
            h = h + o.reshape(B, S, SWA_Q_HEADS * HEAD_DIM) @ swa_w_o[j]
        h = h + FFN_RES_SCALE * swiglu(rms_norm(h, ffn2_norm[layer]), ffn2_w_in[layer], ffn2_w_out[layer])
        if layer == N_A_LAYERS - 1:
            kv = (rms_norm(h, kv_norm) @ kv_w).reshape(B, S, 2, SWA_KV_HEADS, HEAD_DIM)
            k_shared = rotary(kv[:, :, 0], pos)
            v_shared = kv[:, :, 1]
    return rms_norm(h, final_norm)
```

```python
import numpy as np
from contextlib import ExitStack
import concourse.bass as bass
import concourse.mybir as mybir
from concourse.bass_utils import run_bass_kernel_spmd

F32 = mybir.dt.float32
BF16 = mybir.dt.bfloat16
AF = mybir.ActivationFunctionType
ALU = mybir.AluOpType

D = 1024
DFF = 2816
NF = DFF // 128
S = 8192
B = 2
NCORES = 8
T = 2048
EPS = 1e-6
SAME_ENGINE_SYNC = True


class Op:
    __slots__ = ("eng", "fn", "deps", "needed", "sem", "val", "inc", "group")

    def __init__(self, eng, fn, sem, inc, group=False):
        self.eng, self.fn, self.sem, self.inc, self.group = eng, fn, sem, inc, group
        self.deps = []
        self.needed = False
        self.val = 0


class Prog:
    ENGS = ("pe", "act", "dve", "pool", "sp")

    def __init__(self, nc, es):
        self.nc, self.es = nc, es
        self.ops = []
        self.last_w = {}
        self.readers = {}
        self.bank_i = 0
        self.banks = []
        self.barrier_ops = []

    def sb(self, name, shape, dt):
        return self.es.enter_context(self.nc.sbuf_tensor(name, list(shape), dt))

    def alloc_banks(self, n=8):
        for i in range(n):
            self.banks.append(self.es.enter_context(self.nc.psum_tensor(f"bank{i}", [128, 512], F32)))

    def bank(self):
        i = self.bank_i
        self.bank_i = (self.bank_i + 1) % len(self.banks)
        return i

    def add(self, eng, fn, reads=(), writes=(), sem=None, inc=1, group=False):
        op = Op(eng, fn, sem if sem is not None else eng, inc, group)
        deps = {}
        for k in reads:
            w = self.last_w.get(k)
            if w is not None:
                deps[id(w)] = w
        for k in writes:
            w = self.last_w.get(k)
            if w is not None:
                deps[id(w)] = w
            for r in self.readers.get(k, ()):
                deps[id(r)] = r
        for b in self.barrier_ops:
            deps[id(b)] = b
        op.deps = list(deps.values())
        for k in reads:
            self.readers.setdefault(k, []).append(op)
        for k in writes:
            self.last_w[k] = op
            self.readers[k] = []
        self.ops.append(op)
        return op

    def barrier(self):
        last = {}
        for op in self.ops:
            last[(op.eng, op.sem)] = op
        self.barrier_ops = list(last.values())
        self.last_w = {}
        self.readers = {}

    def mm(self, out, lhsT, rhs, start, stop, reads, writes):
        return self.add("pe", lambda e: e.matmul(out, lhsT=lhsT, rhs=rhs, start=start, stop=stop),
                        reads, writes)

    def act(self, out, in_, func, reads, writes, bias=0.0, scale=1.0):
        return self.add("act", lambda e: e.activation(out=out, in_=in_, func=func, bias=bias, scale=scale),
                        reads, writes)

    def tt(self, eng, out, in0, in1, op, reads, writes):
        return self.add(eng, lambda e: e.tensor_tensor(out=out, in0=in0, in1=in1, op=op), reads, writes)

    def stt(self, eng, out, in0, scalar, in1, op0, op1, reads, writes):
        return self.add(eng, lambda e: e.scalar_tensor_tensor(out=out, in0=in0, scalar=scalar, in1=in1,
                                                              op0=op0, op1=op1), reads, writes)

    def ts(self, eng, out, in0, s1, s2, op0, op1, reads, writes):
        return self.add(eng, lambda e: e.tensor_scalar(out=out, in0=in0, scalar1=s1, scalar2=s2,
                                                       op0=op0, op1=op1), reads, writes)

    def copy(self, eng, out, in_, reads, writes):
        if eng == "act":
            return self.add(eng, lambda e: e.copy(out=out, in_=in_), reads, writes)
        return self.add(eng, lambda e: e.tensor_copy(out=out, in_=in_), reads, writes)

    def recip(self, out, in_, reads, writes):
        return self.add("dve", lambda e: e.reciprocal(out=out, in_=in_), reads, writes)

    def memset(self, eng, ap, val, writes):
        return self.add(eng, lambda e: e.memset(ap, val), (), writes)

    def dma(self, q, out, in_, sem, reads, writes, group=False):
        return self.add(q, lambda e: e.dma_start(out=out, in_=in_), reads, writes, sem=sem, inc=16, group=group)

    def emit(self):
        nc = self.nc
        for op in self.ops:
            for d in op.deps:
                d.needed = True
        counts = {}
        totals = {}
        for op in self.ops:
            if op.inc == 16:
                op.needed = True
            if op.needed:
                counts[op.sem] = counts.get(op.sem, 0) + op.inc
                op.val = counts[op.sem]
        totals = dict(counts)
        for op in self.ops:
            if op.group:
                op.val = totals[op.sem]
        sems = {}
        for name in totals:
            sems[name] = self.es.enter_context(nc.semaphore("s_" + name))
        by_eng = {e: [o for o in self.ops if o.eng == e] for e in self.ENGS}
        dma_sems = [n for n in totals if n not in self.ENGS]

        def run(engname, e):
            waited = {}
            for op in by_eng[engname]:
                need = {}
                for d in op.deps:
                    if d.eng == engname and d.sem == engname:
                        if engname == "pe" or not SAME_ENGINE_SYNC:
                            continue
                    if need.get(d.sem, 0) < d.val:
                        need[d.sem] = d.val
                for s, v in need.items():
                    if waited.get(s, 0) < v:
                        e.wait_ge(sems[s], v)
                        waited[s] = v
                ins = op.fn(e)
                if op.needed:
                    ins.then_inc(sems[op.sem], op.inc)
            if engname == "sp":
                for s in dma_sems:
                    e.wait_ge(sems[s], totals[s])
                for s in self.ENGS:
                    if s in totals and s != "sp":
                        e.wait_ge(sems[s], totals[s])

        block = self.es.enter_context(nc.Block())

        @block.tensor
        def _(e):
            run("pe", e)

        @block.scalar
        def _(e):
            run("act", e)

        @block.vector
        def _(e):
            run("dve", e)

        @block.gpsimd
        def _(e):
            run("pool", e)

        @block.sync
        def _(e):
            run("sp", e)


class Tok:
    def __init__(self, P, Tn):
        self.P = P
        self.T = Tn
        self.TH = min(Tn, 1024)
        self.NH = Tn // self.TH
        self.NTT = self.TH // 512
        self.hT = P.sb("hT", [128, 8, Tn], F32)
        self.xn = P.sb("xn", [128, 8, self.TH], BF16)
        self.hff = P.sb("hff", [128, NF, self.TH], BF16)
        self.win = [P.sb(f"win_sb{i}", [128, 2, 8, 256], BF16) for i in range(2)]
        self.wout = [P.sb(f"wout_sb{i}", [128, NF, 256], BF16) for i in range(2)]
        self.sq = [P.sb(f"sq{i}", [128, 512], F32) for i in range(2)]
        self.sg = [P.sb(f"sg{i}", [128, 512], F32) for i in range(2)]
        self.sd = P.sb("sd", [128, 512], F32)
        self.rstd = P.sb("rstd", [128, 512], F32)
        self.ones = P.sb("ones", [128, 128], F32)
        self.gains = P.sb("gains_sb", [128, 8, 8], F32)
        self.win_i = 0
        self.wout_i = 0
        self.sq_i = 0
        self.sg_i = 0
        P.memset("dve", self.ones[:], 1.0, ["ones"])

    def win_slot(self):
        i = self.win_i
        self.win_i ^= 1
        return i

    def wout_slot(self):
        i = self.wout_i
        self.wout_i ^= 1
        return i

    def load_gain(self, which, dram_ap):
        self.P.dma("sp", self.gains[:, which, :], dram_ap, "const", (), [("gain", which)], group=True)

    def load_h(self, xT):
        for c in range(8):
            self.P.dma("sp", self.hT[:, c, :], xT[c * 128:(c + 1) * 128, :], "ldh", (),
                       [("hT", c, tt) for tt in range(self.T // 512)], group=True)

    def store_h(self, outT, sem="sth"):
        for c in range(8):
            self.P.dma("sp", outT[c * 128:(c + 1) * 128, :], self.hT[:, c, :], sem,
                       [("hT", c, tt) for tt in range(self.T // 512)], [], group=True)

    def rmsnorm_half(self, half, which, final=False):
        P = self.P
        for tt in range(self.NTT):
            gt = half * self.NTT + tt
            gs = slice(gt * 512, (gt + 1) * 512)
            ls = slice(tt * 512, (tt + 1) * 512)
            b = P.bank()
            bk = ("bank", b)
            for c in range(8):
                j = self.sq_i
                self.sq_i ^= 1
                sq = self.sq[j]
                P.act(sq[:], self.hT[:, c, gs], AF.Square, [("hT", c, gt)], [("sq", j)])
                P.mm(P.banks[b][:], self.ones[:], sq[:], c == 0, c == 7, [("sq", j), "ones"], [bk])
            P.act(self.sd[:], P.banks[b][:], AF.Sqrt, [bk], ["sd"], bias=EPS, scale=1.0 / D)
            P.recip(self.rstd[:], self.sd[:], ["sd"], ["rstd"])
            for c in range(8):
                if final:
                    P.stt("dve", self.hT[:, c, gs], self.hT[:, c, gs], self.gains[:, which, c:c + 1], self.rstd[:],
                          ALU.mult, ALU.mult, [("hT", c, gt), "rstd", ("gain", which)], [("hT", c, gt)])
                else:
                    P.stt("dve", self.xn[:, c, ls], self.hT[:, c, gs], self.gains[:, which, c:c + 1], self.rstd[:],
                          ALU.mult, ALU.mult, [("hT", c, gt), "rstd", ("gain", which)], [("xn", c, tt)])

    def ffn(self, which_gain, win_d, wout_d):
        P = self.P
        for half in range(self.NH):
            self.rmsnorm_half(half, which_gain)
            for g in range(11):
                s = self.win_slot()
                w = self.win[s]
                P.dma("pool", w[:].rearrange("p a k n -> p (a k n)"), win_d[g], f"win{s}", (), [("win", s)])
                for a in range(2):
                    f = g * 2 + a
                    cs = slice(a * 128, (a + 1) * 128)
                    for tt in range(self.NTT):
                        ls = slice(tt * 512, (tt + 1) * 512)
                        bg, bu = P.bank(), P.bank()
                        for kc in range(8):
                            P.mm(P.banks[bg][:], w[:, 0, kc, cs], self.xn[:, kc, ls], kc == 0, kc == 7,
                                 [("win", s), ("xn", kc, tt)], [("bank", bg)])
                        for kc in range(8):
                            P.mm(P.banks[bu][:], w[:, 1, kc, cs], self.xn[:, kc, ls], kc == 0, kc == 7,
                                 [("win", s), ("xn", kc, tt)], [("bank", bu)])
                        j = self.sg_i
                        self.sg_i ^= 1
                        P.act(self.sg[j][:], P.banks[bg][:], AF.Silu, [("bank", bg)], [("sg", j)])
                        P.tt("dve", self.hff[:, f, ls], self.sg[j][:], P.banks[bu][:], ALU.mult,
                             [("sg", j), ("bank", bu)], [("hff", f, tt)])
            for dcp in range(4):
                s = self.wout_slot()
                w = self.wout[s]
                P.dma("pool", w[:].rearrange("p f n -> p (f n)"), wout_d[dcp], f"wout{s}", (), [("wout", s)])
                for a in range(2):
                    dc = dcp * 2 + a
                    cs = slice(a * 128, (a + 1) * 128)
                    for tt in range(self.NTT):
                        gt = half * self.NTT + tt
                        gs = slice(gt * 512, (gt + 1) * 512)
                        ls = slice(tt * 512, (tt + 1) * 512)
                        b = P.bank()
                        for f in range(NF):
                            P.mm(P.banks[b][:], w[:, f, cs], self.hff[:, f, ls], f == 0, f == NF - 1,
                                 [("wout", s), ("hff", f, tt)], [("bank", b)])
                        P.stt("dve", self.hT[:, dc, gs], P.banks[b][:], 0.5, self.hT[:, dc, gs],
                              ALU.mult, ALU.add, [("bank", b), ("hT", dc, gt)], [("hT", dc, gt)])


def new_prog(es):
    nc = bass.Bass("TRN2", target_bir_lowering=False)
    P = Prog(nc, es)
    P.alloc_banks(8)
    return nc, P


def build_p1(es, Tn=T):
    nc, P = new_prog(es)
    xT = nc.dram_tensor("xT", [D, Tn], F32, kind="ExternalInput").ap()
    gains_d = nc.dram_tensor("gains", [2, 128, 8], F32, kind="ExternalInput").ap()
    win_d = nc.dram_tensor("win", [11, 128, 4096], F32, kind="ExternalInput").ap()
    wout_d = nc.dram_tensor("wout", [4, 128, NF * 256], F32, kind="ExternalInput").ap()
    wqkv_d = nc.dram_tensor("wqkv", [6, 128, 4096], F32, kind="ExternalInput").ap()
    h1T = nc.dram_tensor("h1T", [D, Tn], F32, kind="ExternalOutput").ap()
    qT = nc.dram_tensor("qT", [D, Tn], BF16, kind="ExternalOutput").ap()
    kT = nc.dram_tensor("kT", [D, Tn], BF16, kind="ExternalOutput").ap()
    v = nc.dram_tensor("v", [Tn, D], BF16, kind="ExternalOutput").ap()

    K = Tok(P, Tn)
    qkst = [P.sb(f"qkst{i}", [128, 512], BF16) for i in range(2)]
    vst = [P.sb(f"vst{i}", [128, D], BF16) for i in range(2)]
    K.load_gain(0, gains_d[0])
    K.load_gain(1, gains_d[1])
    K.load_h(xT)
    K.ffn(0, win_d, wout_d)
    K.store_h(h1T)
    emit_qkv(P, K, wqkv_d, qT, kT, v, qkst, vst)
    P.emit()
    return nc


def emit_qkv(P, K, wqkv_d, qT, kT, v, qkst, vst):
    st_i = 0
    vs_i = 0
    for half in range(K.NH):
        K.rmsnorm_half(half, 1)
        for i in range(4):
            s = K.win_slot()
            w = K.win[s]
            P.dma("pool", w[:].rearrange("p a k n -> p (a k n)"), wqkv_d[i], f"win{s}", (), [("win", s)])
            for a in range(2):
                for h2 in range(2):
                    chunk = (i % 2) * 4 + a * 2 + h2
                    cs = slice(h2 * 128, (h2 + 1) * 128)
                    for tt in range(K.NTT):
                        gt = half * K.NTT + tt
                        ls = slice(tt * 512, (tt + 1) * 512)
                        b = P.bank()
                        for kc in range(8):
                            P.mm(P.banks[b][:], w[:, a, kc, cs], K.xn[:, kc, ls], kc == 0, kc == 7,
                                 [("win", s), ("xn", kc, tt)], [("bank", b)])
                        j = st_i
                        st_i ^= 1
                        dst = qT if i < 2 else kT
                        P.act(qkst[j][:], P.banks[b][:], AF.Copy, [("bank", b)], [("qkst", j)],
                              scale=(0.125 if i < 2 else 1.0))
                        P.dma("sp", dst[chunk * 128:(chunk + 1) * 128, gt * 512:(gt + 1) * 512], qkst[j][:],
                              f"stqk{j}", [("qkst", j)], [])
        slots = []
        for i in (4, 5):
            s = K.win_slot()
            P.dma("pool", K.win[s][:].rearrange("p a k n -> p (a k n)"), wqkv_d[i], f"win{s}", (), [("win", s)])
            slots.append(s)
        for tb in range(K.TH // 128):
            gtb = half * (K.TH // 128) + tb
            j = vs_i
            vs_i ^= 1
            for cb in range(4):
                s = slots[cb // 2]
                a = cb % 2
                b = P.bank()
                for kc in range(8):
                    P.mm(P.banks[b][:, 0:256], K.xn[:, kc, tb * 128:(tb + 1) * 128], K.win[s][:, a, kc, :],
                         kc == 0, kc == 7, [("win", s), ("xn", kc, tb // 4)], [("bank", b)])
                P.copy("dve", vst[j][:, cb * 256:(cb + 1) * 256], P.banks[b][:, 0:256], [("bank", b)], [("vst", j, cb)])
            P.dma("sp", v[gtb * 128:(gtb + 1) * 128, :], vst[j][:], f"stv{j}",
                  [("vst", j, cb) for cb in range(4)], [])


def tile_cols(Wm, ncols_tile=512):
    Kd, N = Wm.shape
    nt = N // ncols_tile
    x = Wm.reshape(8, 128, nt, 2, 256)
    x = x.transpose(2, 1, 3, 0, 4)
    return np.ascontiguousarray(x).reshape(nt, 128, 4096)


def tile_win(w_in):
    g = w_in[:, :DFF].reshape(8, 128, 11, 256)
    u = w_in[:, DFF:].reshape(8, 128, 11, 256)
    x = np.stack([g, u], axis=0)
    x = x.transpose(3, 2, 0, 1, 4)
    return np.ascontiguousarray(x).reshape(11, 128, 4096)


def tile_wout(w_out):
    x = w_out.reshape(NF, 128, 4, 256)
    x = x.transpose(2, 1, 0, 3)
    return np.ascontiguousarray(x).reshape(4, 128, NF * 256)


def gain_layout(g):
    return np.ascontiguousarray(g.reshape(8, 128).T)


def build_p2(es, Sn=S, NP=4):
    nc, P = new_prog(es)
    NB = Sn // 128
    NQ = Sn // 512
    q4 = nc.dram_tensor("q4", [NP, 64, Sn], BF16, kind="ExternalInput").ap()
    k4 = nc.dram_tensor("k4", [NP, 64, Sn], BF16, kind="ExternalInput").ap()
    v4 = nc.dram_tensor("v4", [NP, 128, NB * 64], BF16, kind="ExternalInput").ap()
    cst = nc.dram_tensor("cst", [6, 128, 512], BF16, kind="ExternalInput").ap()
    o4 = nc.dram_tensor("o4", [NP, 64, Sn], BF16, kind="ExternalOutput").ap()
    emit_sb(P, NP, Sn, q4, k4, v4, cst, o4)
    P.emit()
    return nc


def emit_sb(P, NP, Sn, q4, k4, v4, cst, o4):
    NB = Sn // 128
    NQ = Sn // 512
    NPP = (NP + 1) // 2
    q_sb = P.sb("q_sb", [128, NPP, Sn], BF16)
    k_sb = P.sb("k_sb", [128, NPP, Sn], BF16)
    v_sb = P.sb("v_sb", [128, NP, NB, 64], BF16)
    o_sb = P.sb("o_sb", [128, NPP, Sn], BF16)
    c_sb = P.sb("c_sb", [128, 6, 512], BF16)
    e_sb = [P.sb(f"e_sb{i}", [128, 512], F32) for i in range(2)]
    sp_sb = [P.sb(f"sp_sb{i}", [128, 512], BF16) for i in range(3)]
    w_sb = [P.sb(f"w_sb{i}", [128, 512], BF16) for i in range(2)]
    R_sb = P.sb("R_sb", [128, 512], F32)
    Rb_sb = [P.sb(f"Rb_sb{i}", [128, 512], BF16) for i in range(2)]
    for i in range(6):
        P.dma("sp", c_sb[:, i, :], cst[i], "const", (), [("c", i)], group=True)
    for p in range(NP):
        r0 = (p % 2) * 64
        P.dma("sp", q_sb[r0:r0 + 64, p // 2, :], q4[p], "ldqk", (), [("q", p)], group=True)
        P.dma("sp", k_sb[r0:r0 + 64, p // 2, :], k4[p], "ldqk", (), [("k", p)], group=True)
        P.dma("sp", v_sb[:, p, :, :].rearrange("s n d -> s (n d)"), v4[p], "ldqk", (), [("v", p)], group=True)
    negtri = c_sb[:, 4, 0:128]
    negones = c_sb[:, 5, 0:128]
    blocks = []
    for p in range(NP):
        for qt in range(NQ):
            nkb = 4 * qt + 4
            for n, kb in enumerate(range(nkb - 1, -1, -1)):
                blocks.append((p, qt, kb, n == 0, kb == 0, kb - 4 * qt))
    st = {"e": 0, "sp": 0, "w": 0, "rb": 0, "za": 0, "zb": 0, "ot": 0}

    def stage1(i):
        p, qt, kb, first, last, rel = blocks[i]
        r0 = (p % 2) * 64
        rs = slice(r0, r0 + 64)
        pp = p // 2
        za = st["za"]
        st["za"] ^= 1
        ej = st["e"]
        st["e"] ^= 1
        sj = st["sp"]
        st["sp"] = (sj + 1) % 3
        blocks[i] = blocks[i] + (sj,)
        qs = slice(qt * 512, (qt + 1) * 512)
        ks = slice(kb * 128, (kb + 1) * 128)
        P.mm(P.banks[za][:], k_sb[rs, pp, ks], q_sb[rs, pp, qs], True, True,
             [("k", p), ("q", p)], [("bank", za)])
        P.act(e_sb[ej][:], P.banks[za][:], AF.Exp, [("bank", za)], [("e", ej)])
        P.act(sp_sb[sj][:], e_sb[ej][:], AF.Ln, [("e", ej)], [("sp", sj)], bias=1.0)
        if rel >= 0:
            P.tt("dve", sp_sb[sj][:], sp_sb[sj][:], c_sb[:, rel, :], ALU.mult, [("sp", sj), ("c", rel)], [("sp", sj)])

    def stage2(i):
        p, qt, kb, first, last, rel, sj = blocks[i]
        r0 = (p % 2) * 64
        rs = slice(r0, r0 + 64)
        pp = p // 2
        qs = slice(qt * 512, (qt + 1) * 512)
        ks = slice(kb * 128, (kb + 1) * 128)
        zb = 2 + st["zb"]
        st["zb"] ^= 1
        wj = st["w"]
        st["w"] ^= 1
        if first:
            st["ot"] ^= 1
        ot = 4 + st["ot"]
        P.mm(P.banks[zb][:], k_sb[rs, pp, ks], q_sb[rs, pp, qs], True, False,
             [("k", p), ("q", p)], [("bank", zb)])
        P.mm(P.banks[zb][:], negtri, sp_sb[sj][:], False, first, [("sp", sj), ("c", 4)], [("bank", zb)])
        if not first:
            rb = st["rb"]
            P.mm(P.banks[zb][:], negones, Rb_sb[rb][:], False, True, [("rb", rb), ("c", 5)], [("bank", zb)])
        P.act(w_sb[wj][:], P.banks[zb][:], AF.Exp, [("bank", zb)], [("w", wj)])
        if rel >= 0:
            P.tt("dve", w_sb[wj][:], w_sb[wj][:], c_sb[:, rel, :], ALU.mult, [("w", wj), ("c", rel)], [("w", wj)])
        P.mm(P.banks[ot][rs, :], v_sb[:, p, kb, :], w_sb[wj][:], first, last,
             [("v", p), ("w", wj)], [("bank", ot)])
        if not last:
            if first:
                P.copy("pool", R_sb[:], sp_sb[sj][:], [("sp", sj)], ["R"])
            else:
                P.tt("pool", R_sb[:], R_sb[:], sp_sb[sj][:], ALU.add, [("sp", sj), "R"], ["R"])
            st["rb"] ^= 1
            rb = st["rb"]
            P.copy("pool", Rb_sb[rb][:], R_sb[:], ["R"], [("rb", rb)])
        else:
            P.copy("dve", o_sb[rs, pp, qs], P.banks[ot][rs, :], [("bank", ot)], [("o", p, qt)])
            P.dma("sp", o4[p][:, qs], o_sb[rs, pp, qs], "sto", [("o", p, qt)], [], group=True)

    n = len(blocks)
    stage1(0)
    for i in range(n):
        if i + 1 < n:
            stage1(i + 1)
        stage2(i)


def sb_consts():
    import ml_dtypes
    c = np.zeros((6, 128, 512), np.float32)
    s = np.arange(128)[:, None]
    t = np.arange(512)[None, :]
    for r in range(4):
        c[r] = ((r * 128 + s) < t)
    j = np.arange(128)[:, None]
    ss = np.arange(128)[None, :]
    c[4, :, :128] = -(j >= ss).astype(np.float32)
    c[5] = -1.0
    return c.astype(ml_dtypes.bfloat16)


def emit_oproj_T(P, K, oT_d, wo_d, tag):
    for half in range(K.NH):
        for fc in range(8):
            P.dma("sp", K.xn[:, fc, :], oT_d[fc * 128:(fc + 1) * 128, half * K.TH:(half + 1) * K.TH],
                  f"ldx{tag}{half}", (), [("xn", fc, tt) for tt in range(K.NTT)], group=True)
        for dcp in range(4):
            s = K.wout_slot()
            w = K.wout[s]
            P.dma("pool", w[:, 0:8, :], wo_d[dcp].rearrange("p (f n) -> p f n", f=8), f"wout{s}", (), [("wout", s)])
            for a in range(2):
                dc = dcp * 2 + a
                cs = slice(a * 128, (a + 1) * 128)
                for tt in range(K.NTT):
                    gt = half * K.NTT + tt
                    gs = slice(gt * 512, (gt + 1) * 512)
                    ls = slice(tt * 512, (tt + 1) * 512)
                    b = P.bank()
                    for fc in range(8):
                        P.mm(P.banks[b][:], w[:, fc, cs], K.xn[:, fc, ls], fc == 0, fc == 7,
                             [("wout", s), ("xn", fc, tt)], [("bank", b)])
                    P.tt("dve", K.hT[:, dc, gs], P.banks[b][:], K.hT[:, dc, gs], ALU.add,
                         [("bank", b), ("hT", dc, gt)], [("hT", dc, gt)])


def emit_rot_proj(P, K, half, w, s, a_main, a_perm, ccols, cs_d, csb, dst_fn, kind):
    for tt in range(K.NTT):
        gt = half * K.NTT + tt
        gs = slice(gt * 512, (gt + 1) * 512)
        ls = slice(tt * 512, (tt + 1) * 512)
        b1, b2 = P.bank(), P.bank()
        for kc in range(8):
            P.mm(P.banks[b1][:], w[:, a_main, kc, ccols], K.xn[:, kc, ls], kc == 0, kc == 7,
                 [("win", s), ("xn", kc, tt)], [("bank", b1)])
        for kc in range(8):
            P.mm(P.banks[b2][:], w[:, a_perm, kc, ccols], K.xn[:, kc, ls], kc == 0, kc == 7,
                 [("win", s), ("xn", kc, tt)], [("bank", b2)])
        P.dma("sp", csb[0][:], cs_d[0][:, gs], "cs0", (), ["cs0"])
        P.dma("sp", csb[1][:], cs_d[1][:, gs], "cs1", (), ["cs1"])
        P.tt("dve", K.sq[0][:], P.banks[b1][:], csb[0][:], ALU.mult, [("bank", b1), "cs0"], [("sq", 0)])
        P.tt("dve", K.sq[1][:], P.banks[b2][:], csb[1][:], ALU.mult, [("bank", b2), "cs1"], [("sq", 1)])
        dst_fn(gt, gs)


def build_p3(es, Tn=T):
    nc, P = new_prog(es)
    h1T = nc.dram_tensor("h1T", [D, Tn], F32, kind="ExternalInput").ap()
    oT = nc.dram_tensor("oT", [D, Tn], BF16, kind="ExternalInput").ap()
    wo_d = nc.dram_tensor("wo", [4, 128, 8 * 256], F32, kind="ExternalInput").ap()
    gains_d = nc.dram_tensor("gains", [2, 128, 8], F32, kind="ExternalInput").ap()
    win_d = nc.dram_tensor("win", [11, 128, 4096], F32, kind="ExternalInput").ap()
    wout_d = nc.dram_tensor("wout", [4, 128, NF * 256], F32, kind="ExternalInput").ap()
    wkv_d = nc.dram_tensor("wkv", [2, 128, 4096], F32, kind="ExternalInput").ap()
    cs_d = nc.dram_tensor("cs", [2, 128, Tn], F32, kind="ExternalInput").ap()
    h3T = nc.dram_tensor("h3T", [D, Tn], F32, kind="ExternalOutput").ap()
    ksh = nc.dram_tensor("ksh", [256, Tn], BF16, kind="ExternalOutput").ap()
    vsh = nc.dram_tensor("vsh", [Tn, 256], BF16, kind="ExternalOutput").ap()

    K = Tok(P, Tn)
    csb = [P.sb(f"csb{i}", [128, 512], F32) for i in range(2)]
    kst = [P.sb(f"kst{i}", [128, 512], BF16) for i in range(2)]
    vst = [P.sb(f"vst{i}", [128, 256], BF16) for i in range(2)]
    K.load_gain(0, gains_d[0])
    K.load_gain(1, gains_d[1])
    K.load_h(h1T)
    emit_oproj_T(P, K, oT, wo_d, "o")
    K.ffn(0, win_d, wout_d)
    K.store_h(h3T)
    emit_kv(P, K, wkv_d, cs_d, csb, kst, vst, ksh, vsh)
    P.emit()
    return nc


def emit_kv(P, K, wkv_d, cs_d, csb, kst, vst, ksh, vsh):
    st = {"k": 0, "v": 0}
    for half in range(K.NH):
        K.rmsnorm_half(half, 1)
        s = K.win_slot()
        w = K.win[s]
        P.dma("pool", w[:].rearrange("p a k n -> p (a k n)"), wkv_d[0], f"win{s}", (), [("win", s)])
        for c in range(2):
            def dst(gt, gs, c=c):
                j = st["k"]
                st["k"] ^= 1
                P.tt("dve", kst[j][:], K.sq[0][:], K.sq[1][:], ALU.add, [("sq", 0), ("sq", 1)], [("kst", j)])
                P.dma("sp", ksh[c * 128:(c + 1) * 128, gs], kst[j][:], f"stk{j}", [("kst", j)], [])
            emit_rot_proj(P, K, half, w, s, 0, 1, slice(c * 128, (c + 1) * 128), cs_d, csb, dst, "k")
        s = K.win_slot()
        w = K.win[s]
        P.dma("pool", w[:].rearrange("p a k n -> p (a k n)"), wkv_d[1], f"win{s}", (), [("win", s)])
        for tb in range(K.TH // 128):
            gtb = half * (K.TH // 128) + tb
            j = st["v"]
            st["v"] ^= 1
            b = P.bank()
            for kc in range(8):
                P.mm(P.banks[b][:, 0:256], K.xn[:, kc, tb * 128:(tb + 1) * 128], w[:, 0, kc, :],
                     kc == 0, kc == 7, [("win", s), ("xn", kc, tb // 4)], [("bank", b)])
            P.copy("dve", vst[j][:], P.banks[b][:, 0:256], [("bank", b)], [("vst", j)])
            P.dma("sp", vsh[gtb * 128:(gtb + 1) * 128, :], vst[j][:], f"stv{j}", [("vst", j)], [])


def perm_half(Wm, nheads):
    Kd = Wm.shape[0]
    x = Wm.reshape(Kd, nheads, 2, 32)[:, :, ::-1, :]
    return np.ascontiguousarray(x).reshape(Kd, nheads * 64)


def rope_tables(pos0, Tn):
    half = 32
    inv = (10000.0 ** (-np.arange(half, dtype=np.float32) / half)).astype(np.float32)
    pos = np.arange(pos0, pos0 + Tn, dtype=np.float32)
    ang = pos[None, :] * inv[:, None]
    cos = np.cos(ang).astype(np.float32)
    sin = np.sin(ang).astype(np.float32)
    c64 = np.concatenate([cos, cos], 0)
    s64 = np.concatenate([-sin, sin], 0)
    return np.ascontiguousarray(np.stack([np.concatenate([c64, c64], 0), np.concatenate([s64, s64], 0)]))


def build_p4(es, Tn=T):
    nc, P = new_prog(es)
    NBK = Tn // 128 + 1
    h3T = nc.dram_tensor("h3T", [D, Tn], F32, kind="ExternalInput").ap()
    kh = nc.dram_tensor("kh", [256, 128 + Tn], BF16, kind="ExternalInput").ap()
    vh = nc.dram_tensor("vh", [128, NBK * 256], BF16, kind="ExternalInput").ap()
    gains_d = nc.dram_tensor("gains", [4, 128, 8], F32, kind="ExternalInput").ap()
    win1 = nc.dram_tensor("win1", [11, 128, 4096], F32, kind="ExternalInput").ap()
    wout1 = nc.dram_tensor("wout1", [4, 128, NF * 256], F32, kind="ExternalInput").ap()
    win2 = nc.dram_tensor("win2", [11, 128, 4096], F32, kind="ExternalInput").ap()
    wout2 = nc.dram_tensor("wout2", [4, 128, NF * 256], F32, kind="ExternalInput").ap()
    wq_d = nc.dram_tensor("wq", [4, 128, 4096], F32, kind="ExternalInput").ap()
    wo_d = nc.dram_tensor("wo", [4, 64, 16 * 256], F32, kind="ExternalInput").ap()
    cs_d = nc.dram_tensor("cs", [2, 128, Tn], F32, kind="ExternalInput").ap()
    sinks_d = nc.dram_tensor("sinks", [1, 16], F32, kind="ExternalInput").ap()
    mk_d = nc.dram_tensor("mk", [3, 128, 512], BF16, kind="ExternalInput").ap()
    outT = nc.dram_tensor("outT", [D, Tn], F32, kind="ExternalOutput").ap()

    K = Tok(P, Tn)
    csb = [P.sb(f"csb{i}", [128, 512], F32) for i in range(2)]
    mk_sb = P.sb("mk_sb", [128, 3, 512], BF16)
    sk_sb = P.sb("sk_sb", [65, 16], F32)
    w_sb = [P.sb(f"w_sb{i}", [128, 512], BF16) for i in range(2)]
    dn_sb = P.sb("dn_sb", [65, 512], F32)
    rd_sb = P.sb("rd_sb", [65, 512], F32)
    bc_sb = P.sb("bc_sb", [64, 512], F32)
    for i in range(4):
        K.load_gain(i, gains_d[i])
    for i in range(3):
        P.dma("sp", mk_sb[:, i, :], mk_d[i], "const", (), [("mk", i)], group=True)
    P.dma("sp", sk_sb[64:65, :], sinks_d[:, :], "const", (), ["sk"], group=True)
    K.load_h(h3T)
    K.ffn(0, win1, wout1)
    P.barrier()
    emit_swa_layer(P, K, wq_d, wo_d, cs_d, csb, kh, vh, mk_sb, sk_sb, w_sb, dn_sb, rd_sb, bc_sb, Tn)
    P.barrier()
    K.ffn(2, win2, wout2)
    for half in range(K.NH):
        K.rmsnorm_half(half, 3, final=True)
    K.store_h(outT, "sto")
    P.emit()
    return nc


def emit_swa_layer(P, K, wq_d, wo_d, cs_d, csb, kh, vh, mk_sb, sk_sb, w_sb, dn_sb, rd_sb, bc_sb, Tn):
    NBK = Tn // 128 + 1
    q_sb = K.hff[:].rearrange("p f t -> p (f t)")[:, 0:8 * Tn].rearrange("p (c t) -> p c t", c=8)
    k_sb = K.wout[0][:].rearrange("p f n -> p (f n)")[:, 0:2 * (128 + Tn)].rearrange("p (c t) -> p c t", c=2)
    v_sb = K.wout[1][:].rearrange("p f n -> p (f n)")[:, 0:NBK * 4 * 65].rearrange("p (b g d) -> p b g d", b=NBK, g=4)
    o_sb = K.xn[:].rearrange("p c t -> p (c t)")[:, 0:16 * 512].rearrange("p (h t) -> p h t", h=16)

    for half in range(K.NH):
        K.rmsnorm_half(half, 1)
        for i in range(4):
            s = K.win_slot()
            w = K.win[s]
            P.dma("pool", w[:].rearrange("p a k n -> p (a k n)"), wq_d[i], f"win{s}", (), [("win", s)])
            for c2 in range(2):
                chunk = 2 * i + c2

                def dst(gt, gs, chunk=chunk):
                    P.tt("dve", q_sb[:, chunk, gs], K.sq[0][:], K.sq[1][:], ALU.add,
                         [("sq", 0), ("sq", 1)], [("q", chunk, gt)])
                emit_rot_proj(P, K, half, w, s, 0, 1, slice(c2 * 128, (c2 + 1) * 128), cs_d, csb, dst, "q")
    P.barrier()
    for c in range(2):
        P.dma("sp", k_sb[:, c, :], kh[c * 128:(c + 1) * 128, :], "ldkv", (), [("k", c)], group=True)
    P.memset("dve", K.wout[1][:], 1.0, ["vones"])
    vh_v = vh.rearrange("p (b g d) -> p b g d", b=NBK, g=4)
    for g in range(4):
        P.dma("sp", v_sb[:, :, g, 0:64], vh_v[:, :, g, :], "ldkv", ["vones"], [("v", g)], group=True)
    P.act(sk_sb[64:65, :], sk_sb[64:65, :], AF.Exp, ["sk"], ["sk"])
    st = {"w": 0}
    for tt in range(Tn // 512):
        gsl = slice(tt * 512, (tt + 1) * 512)
        for ib in range(4):
            i = tt * 4 + ib
            for g in range(4):
                gp, e = g // 2, g % 2
                rs = slice(e * 64, e * 64 + 64)
                ob = P.bank()
                for kbi in range(2):
                    kb = i + kbi
                    sbk = P.bank()
                    for j in range(4):
                        P.mm(P.banks[sbk][:, j * 128:(j + 1) * 128], k_sb[rs, gp, kb * 128:(kb + 1) * 128],
                             q_sb[rs, gp * 4 + j, i * 128:(i + 1) * 128], True, True,
                             [("k", gp), ("q", gp * 4 + j, i // 4)], [("bank", sbk)])
                    wj = st["w"]
                    st["w"] ^= 1
                    P.act(w_sb[wj][:], P.banks[sbk][:], AF.Exp, [("bank", sbk)], [("w", wj)], scale=0.125)
                    mi = 2 if kbi == 1 else (0 if i == 0 else 1)
                    P.tt("dve", w_sb[wj][:], w_sb[wj][:], mk_sb[:, mi, :], ALU.mult, [("w", wj), ("mk", mi)], [("w", wj)])
                    P.mm(P.banks[ob][0:65, :], v_sb[:, kb, g, :], w_sb[wj][:], kbi == 0, kbi == 1,
                         [("v", g), ("w", wj)], [("bank", ob)])
                P.tt("dve", dn_sb[64:65, :].rearrange("p (j t) -> p j t", j=4),
                     P.banks[ob][64:65, :].rearrange("p (j t) -> p j t", j=4),
                     sk_sb[64:65, 4 * g:4 * g + 4].unsqueeze(2).to_broadcast([1, 4, 128]), ALU.add,
                     [("bank", ob), "sk"], ["dn"])
                P.recip(rd_sb[64:65, :], dn_sb[64:65, :], ["dn"], ["rd"])
                bb = P.bank()
                P.mm(P.banks[bb][0:64, :], K.ones[64:65, 0:64], rd_sb[64:65, :], True, True, ["rd", "ones"], [("bank", bb)])
                P.copy("act", bc_sb[:, :], P.banks[bb][0:64, :], [("bank", bb)], ["bc"])
                P.tt("dve", o_sb[0:64, 4 * g:4 * g + 4, ib * 128:(ib + 1) * 128],
                     P.banks[ob][0:64, :].rearrange("p (j t) -> p j t", j=4),
                     bc_sb[:, :].rearrange("p (j t) -> p j t", j=4), ALU.mult,
                     [("bank", ob), "bc"], [("o", g, ib)])
        for dcp in range(4):
            s = K.win_slot()
            wv = K.win[s][:].rearrange("p a k n -> p (a k n)")[0:64, :].rearrange("p (h n) -> p h n", h=16)
            P.dma("pool", K.win[s][:].rearrange("p a k n -> p (a k n)")[0:64, :], wo_d[dcp], f"win{s}", (), [("win", s)])
            for a in range(2):
                dc = dcp * 2 + a
                b = P.bank()
                for h in range(16):
                    P.mm(P.banks[b][:], wv[:, h, a * 128:(a + 1) * 128], o_sb[0:64, h, :], h == 0, h == 15,
                         [("win", s)] + [("o", h // 4, ib) for ib in range(4)], [("bank", b)])
                P.tt("dve", K.hT[:, dc, gsl], P.banks[b][:], K.hT[:, dc, gsl], ALU.add,
                     [("bank", b), ("hT", dc, tt)], [("hT", dc, tt)])


def tile_wo8(Wo):
    x = Wo.reshape(8, 128, 4, 256).transpose(2, 1, 0, 3)
    return np.ascontiguousarray(x).reshape(4, 128, 8 * 256)


def tile_pair(Wa, Wb):
    k = Wa.shape[1] // 256
    xa = Wa.reshape(8, 128, k, 256)
    xb = Wb.reshape(8, 128, k, 256)
    x = np.stack([xa, xb], 0).transpose(3, 2, 0, 1, 4)
    return np.ascontiguousarray(x).reshape(k, 128, 4096)


def swa_masks(seq_start):
    import ml_dtypes
    s = np.arange(128)[:, None]
    t = np.tile(np.arange(128), 4)[None, :]
    prev = (s > t).astype(np.float32)
    cur = (s <= t).astype(np.float32)
    first = np.zeros_like(prev) if seq_start else prev
    return np.stack([first, prev, cur]).astype(ml_dtypes.bfloat16)


_CACHE = {}


def _run(name, builder, in_maps):
    es = ExitStack()
    nc = builder(es)
    res = run_bass_kernel_spmd(nc, in_maps, core_ids=list(range(NCORES)))
    es.close()
    return res.results


def kernel(x, ffn1_norm, ffn1_w_in, ffn1_w_out, mix_norm, ffn2_norm, ffn2_w_in, ffn2_w_out,
           sb_w_qkv, sb_w_o, kv_norm, kv_w, swa_w_q, swa_sinks, swa_w_o, final_norm, _debug=None):
    import ml_dtypes
    f = lambda a: np.asarray(a, dtype=np.float32)
    x = f(x)
    CPS = NCORES // B
    tok = [(c // CPS, (c % CPS) * T) for c in range(NCORES)]

    shared = {
        "gains": np.stack([gain_layout(f(ffn1_norm)[0]), gain_layout(f(mix_norm)[0])]),
        "win": tile_win(f(ffn1_w_in)[0]),
        "wout": tile_wout(f(ffn1_w_out)[0]),
        "wqkv": tile_cols(f(sb_w_qkv)[0]),
    }
    maps = [dict(shared, xT=np.ascontiguousarray(x[b, t0:t0 + T].T)) for (b, t0) in tok]
    r1 = _run("p1", lambda es: build_p1(es, T), maps)
    if _debug is not None:
        _debug["r1"] = r1

    qf = np.empty((B, 16, 64, S), ml_dtypes.bfloat16)
    kf = np.empty((B, 16, 64, S), ml_dtypes.bfloat16)
    vf = np.empty((B, 16, S, 64), ml_dtypes.bfloat16)
    for c, (b, t0) in enumerate(tok):
        qf[b, :, :, t0:t0 + T] = r1[c]["qT"].reshape(16, 64, T)
        kf[b, :, :, t0:t0 + T] = r1[c]["kT"].reshape(16, 64, T)
        vf[b, :, t0:t0 + T, :] = r1[c]["v"].reshape(T, 16, 64).transpose(1, 0, 2)
    qf = qf.reshape(B * 16, 64, S)
    kf = kf.reshape(B * 16, 64, S)
    vf = vf.reshape(B * 16, S // 128, 128, 64).transpose(0, 2, 1, 3).reshape(B * 16, 128, (S // 128) * 64)
    cst = sb_consts()
    NPC = B * 16 // NCORES
    maps = [{"q4": np.ascontiguousarray(qf[c * NPC:(c + 1) * NPC]),
             "k4": np.ascontiguousarray(kf[c * NPC:(c + 1) * NPC]),
             "v4": np.ascontiguousarray(vf[c * NPC:(c + 1) * NPC]), "cst": cst} for c in range(NCORES)]
    r2 = _run("p2", lambda es: build_p2(es, S, NPC), maps)
    of = np.concatenate([r2[c]["o4"] for c in range(NCORES)], 0).reshape(B, 16 * 64, S)
    if _debug is not None:
        _debug["of"] = of

    kvw = f(kv_w)
    Wk, Wv = kvw[:, :256], kvw[:, 256:]
    shared = {
        "wo": tile_wo8(f(sb_w_o)[0]),
        "gains": np.stack([gain_layout(f(ffn2_norm)[0]), gain_layout(f(kv_norm))]),
        "win": tile_win(f(ffn2_w_in)[0]),
        "wout": tile_wout(f(ffn2_w_out)[0]),
        "wkv": np.concatenate([tile_pair(Wk, perm_half(Wk, 4)), tile_pair(Wv, Wv)], 0),
    }
    maps = [dict(shared, h1T=r1[c]["h1T"], oT=np.ascontiguousarray(of[b, :, t0:t0 + T]),
                 cs=rope_tables(t0, T)) for c, (b, t0) in enumerate(tok)]
    r3 = _run("p3", lambda es: build_p3(es, T), maps)
    if _debug is not None:
        _debug["r3"] = r3

    Wq = f(swa_w_q)[0]
    order = []
    for gp in range(2):
        for j in range(4):
            order += [4 * (2 * gp) + j, 4 * (2 * gp + 1) + j]
    Wq2 = np.ascontiguousarray(Wq.reshape(D, 16, 64)[:, order, :]).reshape(D, D)
    Wo1 = f(swa_w_o)[0]
    shared = {
        "gains": np.stack([gain_layout(f(ffn1_norm)[1]), gain_layout(f(mix_norm)[1]),
                           gain_layout(f(ffn2_norm)[1]), gain_layout(f(final_norm))]),
        "win1": tile_win(f(ffn1_w_in)[1]), "wout1": tile_wout(f(ffn1_w_out)[1]),
        "win2": tile_win(f(ffn2_w_in)[1]), "wout2": tile_wout(f(ffn2_w_out)[1]),
        "wq": tile_pair(Wq2, perm_half(Wq2, 16)),
        "wo": np.ascontiguousarray(Wo1.reshape(16, 64, 4, 256).transpose(2, 1, 0, 3)).reshape(4, 64, 16 * 256),
        "sinks": f(swa_sinks)[0][None, :],
    }
    maps = []
    for c, (b, t0) in enumerate(tok):
        if t0 == 0:
            kprev = np.zeros((256, 128), ml_dtypes.bfloat16)
            vprev = np.zeros((128, 256), ml_dtypes.bfloat16)
        else:
            kprev = r3[c - 1]["ksh"][:, -128:]
            vprev = r3[c - 1]["vsh"][-128:, :]
        khm = np.ascontiguousarray(np.concatenate([kprev, r3[c]["ksh"]], 1))
        vhm = np.concatenate([vprev, r3[c]["vsh"]], 0)
        NBK = T // 128 + 1
        vhm = np.ascontiguousarray(vhm.reshape(NBK, 128, 256).transpose(1, 0, 2)).reshape(128, NBK * 256)
        maps.append(dict(shared, h3T=r3[c]["h3T"], kh=khm, vh=vhm, cs=rope_tables(t0, T), mk=swa_masks(t0 == 0)))
    r4 = _run("p4", lambda es: build_p4(es, T), maps)
    out = np.empty((B, S, D), np.float32)
    for c, (b, t0) in enumerate(tok):
        out[b, t0:t0 + T, :] = r4[c]["outT"].T
    return out
```

```python
import numpy as np
from contextlib import ExitStack
import concourse.bass as bass
import concourse.mybir as mybir
from concourse.bass_utils import run_bass_kernel_spmd

F32 = mybir.dt.float32
BF16 = mybir.dt.bfloat16
AF = mybir.ActivationFunctionType
ALU = mybir.AluOpType

D = 1024
DFF = 2816
NF = DFF // 128
S = 8192
B = 2
NCORES = 8
T = 2048
EPS = 1e-6
SAME_ENGINE_SYNC = False
RENG = "dve"
RENG2 = "dve"


class Op:
    __slots__ = ("eng", "fn", "deps", "needed", "sem", "val", "inc", "group")

    def __init__(self, eng, fn, sem, inc, group=None):
        self.eng, self.fn, self.sem, self.inc, self.group = eng, fn, sem, inc, group
        self.deps = []
        self.needed = False
        self.val = 0


class Ctx:
    def __init__(self, nc, es):
        self.nc, self.es = nc, es
        self.sems = {}
        self.totals = {}
        self.banks = [es.enter_context(nc.psum_tensor(f"bank{i}", [128, 512], F32)) for i in range(8)]
        self.nstage = 0


class Prog:
    ENGS = ("pe", "act", "dve", "pool", "sp")

    def __init__(self, ctx, es):
        self.ctx = ctx
        self.nc, self.es = ctx.nc, es
        self.ops = []
        self.last_w = {}
        self.readers = {}
        self.bank_i = 0
        self.banks = ctx.banks
        self.barrier_ops = []
        ctx.nstage += 1
        self.tag = f"g{ctx.nstage}_"

    def sb(self, name, shape, dt):
        return self.es.enter_context(self.nc.sbuf_tensor(self.tag + name, list(shape), dt))

    def bank(self):
        i = self.bank_i
        self.bank_i = (self.bank_i + 1) % len(self.banks)
        return i

    def add(self, eng, fn, reads=(), writes=(), sem=None, inc=1, group=None):
        op = Op(eng, fn, sem if sem is not None else eng, inc, group)
        deps = {}
        for k in reads:
            w = self.last_w.get(k)
            if w is not None:
                deps[id(w)] = w
        for k in writes:
            w = self.last_w.get(k)
            if w is not None:
                deps[id(w)] = w
            for r in self.readers.get(k, ()):
                deps[id(r)] = r
        for b in self.barrier_ops:
            deps[id(b)] = b
        op.deps = [d for d in deps.values()
                   if not (group is not None and d.group == group and d.sem == op.sem)]
        for k in reads:
            self.readers.setdefault(k, []).append(op)
        for k in writes:
            self.last_w[k] = op
            self.readers[k] = []
        self.ops.append(op)
        return op

    def barrier(self):
        last = {}
        for op in self.ops:
            last[(op.eng, op.sem)] = op
        self.barrier_ops = list(last.values())
        self.last_w = {}
        self.readers = {}

    def mm(self, out, lhsT, rhs, start, stop, reads, writes):
        return self.add("pe", lambda e: e.matmul(out, lhsT=lhsT, rhs=rhs, start=start, stop=stop),
                        reads, writes)

    def act(self, out, in_, func, reads, writes, bias=0.0, scale=1.0):
        return self.add("act", lambda e: e.activation(out=out, in_=in_, func=func, bias=bias, scale=scale),
                        reads, writes)

    def tt(self, eng, out, in0, in1, op, reads, writes):
        return self.add(eng, lambda e: e.tensor_tensor(out=out, in0=in0, in1=in1, op=op), reads, writes)

    def stt(self, eng, out, in0, scalar, in1, op0, op1, reads, writes):
        return self.add(eng, lambda e: e.scalar_tensor_tensor(out=out, in0=in0, scalar=scalar, in1=in1,
                                                              op0=op0, op1=op1), reads, writes)

    def ts(self, eng, out, in0, s1, s2, op0, op1, reads, writes):
        return self.add(eng, lambda e: e.tensor_scalar(out=out, in0=in0, scalar1=s1, scalar2=s2,
                                                       op0=op0, op1=op1), reads, writes)

    def copy(self, eng, out, in_, reads, writes):
        if eng == "act":
            return self.add(eng, lambda e: e.copy(out=out, in_=in_), reads, writes)
        return self.add(eng, lambda e: e.tensor_copy(out=out, in_=in_), reads, writes)

    def recip(self, out, in_, reads, writes):
        return self.add("dve", lambda e: e.reciprocal(out=out, in_=in_), reads, writes)

    def memset(self, eng, ap, val, writes):
        return self.add(eng, lambda e: e.memset(ap, val), (), writes)

    def dma(self, q, out, in_, sem, reads, writes, group=None):
        return self.add(q, lambda e: e.dma_start(out=out, in_=in_), reads, writes, sem=sem, inc=16, group=group)

    def emit(self):
        nc, ctx = self.nc, self.ctx
        for op in self.ops:
            for d in op.deps:
                d.needed = True
        last_of = {}
        for op in self.ops:
            last_of[op.eng] = op
        for op in last_of.values():
            op.needed = True
        counts = dict(ctx.totals)
        for op in self.ops:
            if op.inc == 16:
                op.needed = True
            if op.needed:
                counts[op.sem] = counts.get(op.sem, 0) + op.inc
                op.val = counts[op.sem]
        gmax = {}
        for op in self.ops:
            if op.group is not None:
                k = (op.sem, op.group)
                gmax[k] = max(gmax.get(k, 0), op.val)
        for op in self.ops:
            if op.group is not None:
                op.val = gmax[(op.sem, op.group)]
        for name in counts:
            if name not in ctx.sems:
                ctx.sems[name] = ctx.es.enter_context(nc.semaphore("s_" + name))
        ctx.totals = counts
        sems, totals = ctx.sems, counts
        by_eng = {e: [o for o in self.ops if o.eng == e] for e in self.ENGS}

        def run(engname, e):
            waited = {}
            for op in by_eng[engname]:
                need = {}
                for d in op.deps:
                    if d.eng == engname and d.sem == engname:
                        if engname == "pe" or not SAME_ENGINE_SYNC:
                            continue
                    if need.get(d.sem, 0) < d.val:
                        need[d.sem] = d.val
                for sname, v in need.items():
                    if waited.get(sname, 0) < v:
                        e.wait_ge(sems[sname], v)
                        waited[sname] = v
                ins = op.fn(e)
                if op.needed:
                    ins.then_inc(sems[op.sem], op.inc)
            for sname, v in totals.items():
                if v > 0 and sname != engname:
                    e.wait_ge(sems[sname], v)

        block = self.es.enter_context(nc.Block())

        @block.tensor
        def _(e):
            run("pe", e)

        @block.scalar
        def _(e):
            run("act", e)

        @block.vector
        def _(e):
            run("dve", e)

        @block.gpsimd
        def _(e):
            run("pool", e)

        @block.sync
        def _(e):
            run("sp", e)


class Tok:
    def __init__(self, P, Tn, TH=None):
        self.P = P
        self.T = Tn
        self.TH = TH if TH is not None else min(Tn, 1024)
        self.NH = Tn // self.TH
        self.tiles = [(s0, min(512, self.TH - s0)) for s0 in range(0, self.TH, 512)]
        self.NTT = len(self.tiles)
        self.hT = P.sb("hT", [128, 8, Tn], F32)
        self.hname = "hT"
        self.hbufs = [(self.hT, "hT")]
        self.xn = P.sb("xn", [128, 8, self.TH], BF16)
        self.hff = P.sb("hff", [128, NF, self.TH], BF16)
        self.win = [P.sb(f"win_sb{i}", [128, 2, 8, 256], BF16) for i in range(2)]
        self.wout = [P.sb(f"wout_sb{i}", [128, NF, 256], BF16) for i in range(2)]
        self.sq = [P.sb(f"sq{i}", [128, 512], F32) for i in range(2)]
        self.sg = [P.sb(f"sg{i}", [128, 512], F32) for i in range(2)]
        self.sd = P.sb("sd", [128, 512], F32)
        self.rstd = P.sb("rstd", [128, 512], F32)
        self.ones = P.sb("ones", [128, 128], F32)
        self.gains = P.sb("gains_sb", [128, 8, 8], F32)
        self.win_i = 0
        self.wout_i = 0
        self.sq_i = 0
        self.sg_i = 0
        self.uid = 0
        P.memset("dve", self.ones[:], 1.0, ["ones"])

    def add_hbuf(self):
        t = self.P.sb("hT2", [128, 8, self.T], F32)
        self.hbufs.append((t, "hT2"))

    def use_hbuf(self, i):
        self.hT, self.hname = self.hbufs[i % len(self.hbufs)]

    def win_slot(self):
        i = self.win_i
        self.win_i ^= 1
        return i

    def wout_slot(self):
        i = self.wout_i
        self.wout_i ^= 1
        return i

    def gid(self):
        self.uid += 1
        return self.uid

    def all_tiles(self):
        return [(h, tt) for h in range(self.NH) for tt in range(self.NTT)]

    def load_gains(self, gains_d, n):
        for w in range(n):
            self.P.dma("sp", self.gains[:, w, :], gains_d[w], "const", (), [("gain", w)], group="c")

    def load_h(self, srcs):
        g = self.gid()
        keys = [(self.hname, c, gt) for c in range(8) for gt in range(self.NH * self.NTT)]
        for (src, off) in srcs:
            n = src.shape[1]
            for c in range(8):
                self.P.dma("sp", self.hT[:, c, off:off + n], src[c * 128:(c + 1) * 128, :], "ld" + self.hname, (),
                           [(self.hname, c, gt) for gt in range(self.NH * self.NTT)], group=g)

    def store_h(self, dsts, sem="sth"):
        g = self.gid()
        for (dst, off) in dsts:
            n = dst.shape[1]
            for c in range(8):
                self.P.dma("sp", dst[c * 128:(c + 1) * 128, :], self.hT[:, c, off:off + n], sem,
                           [(self.hname, c, gt) for gt in range(self.NH * self.NTT)], [], group=g)

    def rmsnorm_half(self, half, which, final=False):
        P = self.P
        for tt, (t0, tw) in enumerate(self.tiles):
            gt = half * self.NTT + tt
            gs = slice(half * self.TH + t0, half * self.TH + t0 + tw)
            ls = slice(t0, t0 + tw)
            b = P.bank()
            bk = ("bank", b)
            for c in range(8):
                j = self.sq_i
                self.sq_i ^= 1
                sq = self.sq[j]
                P.act(sq[:, 0:tw], self.hT[:, c, gs], AF.Square, [(self.hname, c, gt)], [("sq", j)])
                P.mm(P.banks[b][:, 0:tw], self.ones[:], sq[:, 0:tw], c == 0, c == 7, [("sq", j), "ones"], [bk])
            P.act(self.sd[:, 0:tw], P.banks[b][:, 0:tw], AF.Sqrt, [bk], ["sd"], bias=EPS, scale=1.0 / D)
            P.recip(self.rstd[:, 0:tw], self.sd[:, 0:tw], ["sd"], ["rstd"])
            for c in range(8):
                if final:
                    P.stt("dve", self.hT[:, c, gs], self.hT[:, c, gs], self.gains[:, which, c:c + 1],
                          self.rstd[:, 0:tw], ALU.mult, ALU.mult,
                          [(self.hname, c, gt), "rstd", ("gain", which)], [(self.hname, c, gt)])
                else:
                    P.stt("dve", self.xn[:, c, ls], self.hT[:, c, gs], self.gains[:, which, c:c + 1],
                          self.rstd[:, 0:tw], ALU.mult, ALU.mult,
                          [(self.hname, c, gt), "rstd", ("gain", which)], [("xn", c, tt)])

    def ffn(self, which_gain, win_d, wout_d):
        P = self.P
        for half in range(self.NH):
            self.rmsnorm_half(half, which_gain)
            for g in range(11):
                s = self.win_slot()
                w = self.win[s]
                P.dma("pool", w[:].rearrange("p a k n -> p (a k n)"), win_d[g], f"win{s}", (), [("win", s)])
                for a in range(2):
                    f = g * 2 + a
                    cs = slice(a * 128, (a + 1) * 128)
                    for tt, (t0, tw) in enumerate(self.tiles):
                        ls = slice(t0, t0 + tw)
                        bg, bu = P.bank(), P.bank()
                        for kc in range(8):
                            P.mm(P.banks[bg][:, 0:tw], w[:, 0, kc, cs], self.xn[:, kc, ls], kc == 0, kc == 7,
                                 [("win", s), ("xn", kc, tt)], [("bank", bg)])
                        for kc in range(8):
                            P.mm(P.banks[bu][:, 0:tw], w[:, 1, kc, cs], self.xn[:, kc, ls], kc == 0, kc == 7,
                                 [("win", s), ("xn", kc, tt)], [("bank", bu)])
                        j = self.sg_i
                        self.sg_i ^= 1
                        P.act(self.sg[j][:, 0:tw], P.banks[bg][:, 0:tw], AF.Silu, [("bank", bg)], [("sg", j)])
                        P.tt("dve", self.hff[:, f, ls], self.sg[j][:, 0:tw], P.banks[bu][:, 0:tw], ALU.mult,
                             [("sg", j), ("bank", bu)], [("hff", f, tt)])
            for dcp in range(4):
                s = self.wout_slot()
                w = self.wout[s]
                P.dma("pool", w[:].rearrange("p f n -> p (f n)"), wout_d[dcp], f"wout{s}", (), [("wout", s)])
                for a in range(2):
                    dc = dcp * 2 + a
                    cs = slice(a * 128, (a + 1) * 128)
                    for tt, (t0, tw) in enumerate(self.tiles):
                        gt = half * self.NTT + tt
                        gs = slice(half * self.TH + t0, half * self.TH + t0 + tw)
                        ls = slice(t0, t0 + tw)
                        b = P.bank()
                        for f in range(NF):
                            P.mm(P.banks[b][:, 0:tw], w[:, f, cs], self.hff[:, f, ls], f == 0, f == NF - 1,
                                 [("wout", s), ("hff", f, tt)], [("bank", b)])
                        P.stt("dve", self.hT[:, dc, gs], P.banks[b][:, 0:tw], 0.5, self.hT[:, dc, gs],
                              ALU.mult, ALU.add, [("bank", b), (self.hname, dc, gt)], [(self.hname, dc, gt)])


def emit_qkv(P, K, gain_i, wqkv_d, q_dst, k_dst, v_dst, qkst, vst, st, want_q=True):
    K.rmsnorm_half(0, gain_i)
    for i in range(4):
        if i < 2 and not want_q:
            continue
        s = K.win_slot()
        w = K.win[s]
        P.dma("pool", w[:].rearrange("p a k n -> p (a k n)"), wqkv_d[i], f"win{s}", (), [("win", s)])
        for a in range(2):
            for h2 in range(2):
                chunk = (i % 2) * 4 + a * 2 + h2
                cs = slice(h2 * 128, (h2 + 1) * 128)
                for tt, (t0, tw) in enumerate(K.tiles):
                    ls = slice(t0, t0 + tw)
                    b = P.bank()
                    for kc in range(8):
                        P.mm(P.banks[b][:, 0:tw], w[:, a, kc, cs], K.xn[:, kc, ls], kc == 0, kc == 7,
                             [("win", s), ("xn", kc, tt)], [("bank", b)])
                    j = st["qk"]
                    st["qk"] ^= 1
                    dst = q_dst if i < 2 else k_dst
                    P.act(qkst[j][:, 0:tw], P.banks[b][:, 0:tw], AF.Copy, [("bank", b)], [("qkst", j)],
                          scale=(0.125 if i < 2 else 1.0))
                    P.dma("sp", dst[chunk * 128:(chunk + 1) * 128, t0:t0 + tw], qkst[j][:, 0:tw],
                          f"stqk{j}", [("qkst", j)], [])
    slots = []
    for i in (4, 5):
        s = K.win_slot()
        P.dma("pool", K.win[s][:].rearrange("p a k n -> p (a k n)"), wqkv_d[i], f"win{s}", (), [("win", s)])
        slots.append(s)
    for tb in range(K.TH // 128):
        j = st["v"]
        st["v"] ^= 1
        for cb in range(4):
            s = slots[cb // 2]
            a = cb % 2
            b = P.bank()
            for kc in range(8):
                P.mm(P.banks[b][:, 0:256], K.xn[:, kc, tb * 128:(tb + 1) * 128], K.win[s][:, a, kc, :],
                     kc == 0, kc == 7, [("win", s), ("xn", kc, tb // 4)], [("bank", b)])
            P.copy("dve", vst[j][:, cb * 256:(cb + 1) * 256], P.banks[b][:, 0:256], [("bank", b)], [("vst", j, cb)])
        P.dma("sp", v_dst[tb * 128:(tb + 1) * 128, :], vst[j][:], f"stv{j}",
              [("vst", j, cb) for cb in range(4)], [])


def emit_oproj_T(P, K, o_srcs, wo_d):
    g = K.gid()
    for (src, off) in o_srcs:
        n = src.shape[1]
        for fc in range(8):
            P.dma("sp", K.xn[:, fc, off:off + n], src[fc * 128:(fc + 1) * 128, :],
                  "ldx", (), [("xn", fc, tt) for tt in range(K.NTT)], group=g)
    for dcp in range(4):
        s = K.wout_slot()
        w = K.wout[s]
        P.dma("pool", w[:, 0:8, :], wo_d[dcp].rearrange("p (f n) -> p f n", f=8), f"wout{s}", (), [("wout", s)])
        for a in range(2):
            dc = dcp * 2 + a
            cs = slice(a * 128, (a + 1) * 128)
            for tt, (t0, tw) in enumerate(K.tiles):
                ls = slice(t0, t0 + tw)
                b = P.bank()
                for fc in range(8):
                    P.mm(P.banks[b][:, 0:tw], w[:, fc, cs], K.xn[:, fc, ls], fc == 0, fc == 7,
                         [("wout", s), ("xn", fc, tt)], [("bank", b)])
                P.tt("dve", K.hT[:, dc, ls], P.banks[b][:, 0:tw], K.hT[:, dc, ls], ALU.add,
                     [("bank", b), (K.hname, dc, tt)], [(K.hname, dc, tt)])


def emit_rot_proj(P, K, half, w, s, a_main, a_perm, ccols, cs_d, csb, dst_fn):
    for tt, (t0, tw) in enumerate(K.tiles):
        gt = half * K.NTT + tt
        gs = slice(half * K.TH + t0, half * K.TH + t0 + tw)
        ls = slice(t0, t0 + tw)
        b1, b2 = P.bank(), P.bank()
        for kc in range(8):
            P.mm(P.banks[b1][:, 0:tw], w[:, a_main, kc, ccols], K.xn[:, kc, ls], kc == 0, kc == 7,
                 [("win", s), ("xn", kc, tt)], [("bank", b1)])
        for kc in range(8):
            P.mm(P.banks[b2][:, 0:tw], w[:, a_perm, kc, ccols], K.xn[:, kc, ls], kc == 0, kc == 7,
                 [("win", s), ("xn", kc, tt)], [("bank", b2)])
        P.dma("sp", csb[0][:, 0:tw], cs_d[0][:, gs], "cs0", (), ["cs0"])
        P.dma("sp", csb[1][:, 0:tw], cs_d[1][:, gs], "cs1", (), ["cs1"])
        P.tt("dve", K.sq[0][:, 0:tw], P.banks[b1][:, 0:tw], csb[0][:, 0:tw], ALU.mult, [("bank", b1), "cs0"], [("sq", 0)])
        P.tt("dve", K.sq[1][:, 0:tw], P.banks[b2][:, 0:tw], csb[1][:, 0:tw], ALU.mult, [("bank", b2), "cs1"], [("sq", 1)])
        dst_fn(gt, gs, tw)


def emit_kv(P, K, gain_i, wkv_d, cs_d, csb, kst, vst, ksh, vsh, st):
    K.rmsnorm_half(0, gain_i)
    s = K.win_slot()
    w = K.win[s]
    P.dma("pool", w[:].rearrange("p a k n -> p (a k n)"), wkv_d[0], f"win{s}", (), [("win", s)])
    for c in range(2):
        def dst(gt, gs, tw, c=c):
            j = st["k"]
            st["k"] ^= 1
            P.tt("dve", kst[j][:, 0:tw], K.sq[0][:, 0:tw], K.sq[1][:, 0:tw], ALU.add, [("sq", 0), ("sq", 1)], [("kst", j)])
            P.dma("sp", ksh[c * 128:(c + 1) * 128, gs], kst[j][:, 0:tw], f"stk{j}", [("kst", j)], [])
        emit_rot_proj(P, K, 0, w, s, 0, 1, slice(c * 128, (c + 1) * 128), cs_d, csb, dst)
    s = K.win_slot()
    w = K.win[s]
    P.dma("pool", w[:].rearrange("p a k n -> p (a k n)"), wkv_d[1], f"win{s}", (), [("win", s)])
    for tb in range(K.TH // 128):
        j = st["v"]
        st["v"] ^= 1
        b = P.bank()
        for kc in range(8):
            P.mm(P.banks[b][:, 0:256], K.xn[:, kc, tb * 128:(tb + 1) * 128], w[:, 0, kc, :],
                 kc == 0, kc == 7, [("win", s), ("xn", kc, tb // 4)], [("bank", b)])
        P.copy("dve", vst[j][:, 0:256], P.banks[b][:, 0:256], [("bank", b)], [("vst", j)])
        P.dma("sp", vsh[tb * 128:(tb + 1) * 128, :], vst[j][:, 0:256], f"stv{j}", [("vst", j)], [])


class SBState:
    def __init__(self, P):
        self.c_sb = P.sb("c_sb", [128, 6, 512], BF16)
        self.e_sb = [P.sb(f"e_sb{i}", [128, 512], F32) for i in range(2)]
        self.sp_sb = [P.sb(f"sp_sb{i}", [128, 512], BF16) for i in range(4)]
        self.w_sb = [P.sb(f"w_sb{i}", [128, 512], BF16) for i in range(2)]
        self.R_sb = P.sb("R_sb", [128, 512], F32)
        self.Rb_sb = [P.sb(f"Rb_sb{i}", [128, 512], BF16) for i in range(2)]
        self.nblk = 0
        self.nitem = 0

    def load_consts(self, P, cst):
        for i in range(6):
            P.dma("sp", self.c_sb[:, i, :], cst[i], "const", (), [("c", i)], group="c")


def emit_sb_sweep(P, S_, items):
    c_sb, e_sb, sp_sb, w_sb, R_sb, Rb_sb = S_.c_sb, S_.e_sb, S_.sp_sb, S_.w_sb, S_.R_sb, S_.Rb_sb
    negtri = c_sb[:, 4, 0:128]
    negones = c_sb[:, 5, 0:128]
    blocks = []
    for t, it in enumerate(items):
        nb = len(it["blocks"])
        for m, (kb, W, Wold, mask) in enumerate(it["blocks"]):
            blocks.append((it, kb, W, Wold, mask, m == nb - 1, m, S_.nitem + t))
    S_.nitem += len(items)
    n = len(blocks)
    if n == 0:
        return
    base = S_.nblk
    S_.nblk += n

    def A(i):
        it, kb, W, Wold, mask, last, m, t = blocks[i]
        g = base + i
        za = g % 2
        kk, qk, vk = it["keys"]
        P.mm(P.banks[za][:, 0:W], it["k"][:, kb * 128:(kb + 1) * 128], it["q"][:, 0:W], True, True,
             [kk, qk], [("bank", za)])

    def E(i):
        it, kb, W, Wold, mask, last, m, t = blocks[i]
        g = base + i
        P.act(e_sb[g % 2][:, 0:W], P.banks[g % 2][:, 0:W], AF.Exp, [("bank", g % 2)], [("e", g % 2)])

    def L(i):
        it, kb, W, Wold, mask, last, m, t = blocks[i]
        g = base + i
        sj = g % 4
        P.act(sp_sb[sj][:, 0:W], e_sb[g % 2][:, 0:W], AF.Ln, [("e", g % 2)], [("sp", sj)], bias=1.0)
        if mask is not None:
            c0, c1, rel = mask
            P.tt("dve", sp_sb[sj][:, c0:c1], sp_sb[sj][:, c0:c1], c_sb[:, rel, 0:c1 - c0], ALU.mult,
                 [("sp", sj), ("c", rel)], [("sp", sj)])

    def ZB(i):
        it, kb, W, Wold, mask, last, m, t = blocks[i]
        g = base + i
        sj = g % 4
        zb = 2 + g % 2
        kk, qk, vk = it["keys"]
        P.mm(P.banks[zb][:, 0:W], it["k"][:, kb * 128:(kb + 1) * 128], it["q"][:, 0:W], True, False,
             [kk, qk], [("bank", zb)])
        P.mm(P.banks[zb][:, 0:W], negtri, sp_sb[sj][:, 0:W], False, Wold == 0, [("sp", sj), ("c", 4)], [("bank", zb)])
        if Wold > 0:
            rb = (m - 1) % 2
            P.mm(P.banks[zb][:, 0:Wold], negones, Rb_sb[rb][:, 0:Wold], False, True,
                 [("rb", rb), ("c", 5)], [("bank", zb)])

    def Wx(i):
        it, kb, W, Wold, mask, last, m, t = blocks[i]
        g = base + i
        zb = 2 + g % 2
        wj = g % 2
        P.act(w_sb[wj][:, 0:W], P.banks[zb][:, 0:W], AF.Exp, [("bank", zb)], [("w", wj)])
        if mask is not None:
            c0, c1, rel = mask
            P.tt("dve", w_sb[wj][:, c0:c1], w_sb[wj][:, c0:c1], c_sb[:, rel, 0:c1 - c0], ALU.mult,
                 [("w", wj), ("c", rel)], [("w", wj)])
        if Wold == 0 and W < it["Wmax"]:
            P.memset("pool", w_sb[wj][:, W:it["Wmax"]], 0.0, [("w", wj)])

    def PV(i):
        it, kb, W, Wold, mask, last, m, t = blocks[i]
        g = base + i
        kk, qk, vk = it["keys"]
        ot = 4 + t % 2
        Wm = it["Wmax"]
        if Wold == 0:
            P.mm(P.banks[ot][:, 0:Wm], it["v"](kb), w_sb[g % 2][:, 0:Wm], True, last,
                 [vk, ("w", g % 2)], [("bank", ot)])
        else:
            P.mm(P.banks[ot][:, 0:W], it["v"](kb), w_sb[g % 2][:, 0:W], False, last,
                 [vk, ("w", g % 2)], [("bank", ot)])
        if last:
            it["out"](ot, ("bank", ot))

    def Rx(i):
        it, kb, W, Wold, mask, last, m, t = blocks[i]
        g = base + i
        sj = g % 4
        if last:
            return
        if W > Wold:
            P.copy(RENG, R_sb[:, Wold:W], sp_sb[sj][:, Wold:W], [("sp", sj)], ["R"])
        if Wold > 0:
            P.tt(RENG, R_sb[:, 0:Wold], R_sb[:, 0:Wold], sp_sb[sj][:, 0:Wold], ALU.add, [("sp", sj), "R"], ["R"])
            P.copy(RENG2, Rb_sb[m % 2][:, 0:W], R_sb[:, 0:W], ["R"], [("rb", m % 2)])
        else:
            P.copy(RENG2, Rb_sb[m % 2][:, 0:W], sp_sb[sj][:, 0:W], [("sp", sj)], [("rb", m % 2)])

    A(0)
    for k in range(n + 3):
        if 0 <= k - 2 < n:
            ZB(k - 2)
        if 0 <= k - 3 < n:
            PV(k - 3)
        if k + 1 < n:
            A(k + 1)
        if k < n:
            E(k)
            L(k)
        if 0 <= k - 2 < n:
            Wx(k - 2)
        if 0 <= k - 1 < n:
            Rx(k - 1)


def emit_swa_layer(P, K, gain_i, wq_d, wo_d, cs_d, csb, kh, vh, NBK, kbase, mk_sb, sk_sb, w_sb, dn_sb, rd_sb, bc_sb):
    Tn = K.T
    KW = NBK * 128
    q_sb = K.hff[:].rearrange("p f t -> p (f t)")[:, 0:8 * Tn].rearrange("p (c t) -> p c t", c=8)
    k_sb = K.wout[0][:].rearrange("p f n -> p (f n)")[:, 0:2 * KW].rearrange("p (c t) -> p c t", c=2)
    v_sb = K.wout[1][:].rearrange("p f n -> p (f n)")[:, 0:NBK * 4 * 65].rearrange("p (b g d) -> p b g d", b=NBK, g=4)
    o_sb = K.xn[:].rearrange("p c t -> p (c t)")[:, 0:16 * 512].rearrange("p (h t) -> p h t", h=16)

    for half in range(K.NH):
        K.rmsnorm_half(half, gain_i)
        for i in range(4):
            s = K.win_slot()
            w = K.win[s]
            P.dma("pool", w[:].rearrange("p a k n -> p (a k n)"), wq_d[i], f"win{s}", (), [("win", s)])
            for c2 in range(2):
                chunk = 2 * i + c2

                def dst(gt, gs, tw, chunk=chunk):
                    P.tt("dve", q_sb[:, chunk, gs], K.sq[0][:, 0:tw], K.sq[1][:, 0:tw], ALU.add,
                         [("sq", 0), ("sq", 1)], [("q", chunk, gt)])
                emit_rot_proj(P, K, half, w, s, 0, 1, slice(c2 * 128, (c2 + 1) * 128), cs_d, csb, dst)
    P.barrier()
    for c in range(2):
        P.dma("sp", k_sb[:, c, :], kh[c * 128:(c + 1) * 128, :], "ldkv", (), [("k", c)], group="kv")
    P.memset("dve", K.wout[1][:], 1.0, ["vones"])
    vh_v = vh.rearrange("(b s) (g d) -> s b g d", s=128, g=4)
    for g in range(4):
        P.dma("sp", v_sb[:, :, g, 0:64], vh_v[:, :, g, :], "ldkv", ["vones"], [("v", g)], group="kv")
    P.act(sk_sb[64:65, :], sk_sb[64:65, :], AF.Exp, ["sk"], ["sk"])
    st = {"w": 0}
    for tt in range(Tn // 512):
        gsl = slice(tt * 512, (tt + 1) * 512)
        for ib in range(4):
            i = tt * 4 + ib
            for g in range(4):
                gp, e = g // 2, g % 2
                rs = slice(e * 64, e * 64 + 64)
                ob = P.bank()
                for kbi in range(2):
                    kb = kbase[tt] + ib + kbi
                    sbk = P.bank()
                    for j in range(4):
                        P.mm(P.banks[sbk][:, j * 128:(j + 1) * 128], k_sb[rs, gp, kb * 128:(kb + 1) * 128],
                             q_sb[rs, gp * 4 + j, i * 128:(i + 1) * 128], True, True,
                             [("k", gp), ("q", gp * 4 + j, tt)], [("bank", sbk)])
                    wj = st["w"]
                    st["w"] ^= 1
                    P.act(w_sb[wj][:], P.banks[sbk][:], AF.Exp, [("bank", sbk)], [("w", wj)], scale=0.125)
                    mi = 2 if kbi == 1 else (0 if i == 0 else 1)
                    P.tt("dve", w_sb[wj][:], w_sb[wj][:], mk_sb[:, mi, :], ALU.mult, [("w", wj), ("mk", mi)], [("w", wj)])
                    P.mm(P.banks[ob][0:65, :], v_sb[:, kb, g, :], w_sb[wj][:], kbi == 0, kbi == 1,
                         [("v", g), ("w", wj)], [("bank", ob)])
                P.tt("dve", dn_sb[64:65, :].rearrange("p (j t) -> p j t", j=4),
                     P.banks[ob][64:65, :].rearrange("p (j t) -> p j t", j=4),
                     sk_sb[64:65, 4 * g:4 * g + 4].unsqueeze(2).to_broadcast([1, 4, 128]), ALU.add,
                     [("bank", ob), "sk"], ["dn"])
                P.act(rd_sb[64:65, :], dn_sb[64:65, :], AF.Ln, ["dn"], ["rd"])
                P.act(rd_sb[64:65, :], rd_sb[64:65, :], AF.Exp, ["rd"], ["rd"], scale=-1.0)
                bb = P.bank()
                P.mm(P.banks[bb][0:64, :], K.ones[64:65, 0:64], rd_sb[64:65, :], True, True, ["rd", "ones"], [("bank", bb)])
                P.copy("act", bc_sb[:, :], P.banks[bb][0:64, :], [("bank", bb)], ["bc"])
                P.tt("dve", o_sb[0:64, 4 * g:4 * g + 4, ib * 128:(ib + 1) * 128],
                     P.banks[ob][0:64, :].rearrange("p (j t) -> p j t", j=4),
                     bc_sb[:, :].rearrange("p (j t) -> p j t", j=4), ALU.mult,
                     [("bank", ob), "bc"], [("o", g, ib)])
        for dcp in range(4):
            s = K.win_slot()
            wv = K.win[s][:].rearrange("p a k n -> p (a k n)")[0:64, :].rearrange("p (h n) -> p h n", h=16)
            P.dma("pool", K.win[s][:].rearrange("p a k n -> p (a k n)")[0:64, :], wo_d[dcp], f"win{s}", (), [("win", s)])
            for a in range(2):
                dc = dcp * 2 + a
                b = P.bank()
                for h in range(16):
                    P.mm(P.banks[b][:], wv[:, h, a * 128:(a + 1) * 128], o_sb[0:64, h, :], h == 0, h == 15,
                         [("win", s)] + [("o", h // 4, ib) for ib in range(4)], [("bank", b)])
                P.tt("dve", K.hT[:, dc, gsl], P.banks[b][:], K.hT[:, dc, gsl], ALU.add,
                     [("bank", b), (K.hname, dc, tt)], [(K.hname, dc, tt)])


VS = 8192
NSEG = 4
SEGW = 640
TQ = NSEG * SEGW


def seg_start(j):
    return (4 * j + 3) * 512 - 128


STAGES = ("1", "2a", "2b", "3")
DBG_NHP = 8
DBG_HALO = True
DBG_OWN = True
DBG_LD = 7


def _stage(name):
    if name in STAGES:
        with ExitStack() as ses:
            yield ses


def build_fused(es):
    nc = bass.Bass("TRN2", target_bir_lowering=False)
    ctx = Ctx(nc, es)
    dt = nc.dram_tensor
    xv = dt("xv", [D, VS], F32, kind="ExternalInput").ap()
    gains_d = dt("gains", [8, 128, 8], F32, kind="ExternalInput").ap()
    w_f1a = dt("w_f1a", [11, 128, 4096], F32, kind="ExternalInput").ap()
    w_f1b = dt("w_f1b", [4, 128, NF * 256], F32, kind="ExternalInput").ap()
    wqkv_d = dt("wqkv", [6, 128, 4096], F32, kind="ExternalInput").ap()
    wo0_d = dt("wo0", [4, 128, 8 * 256], F32, kind="ExternalInput").ap()
    w_f2a = dt("w_f2a", [11, 128, 4096], F32, kind="ExternalInput").ap()
    w_f2b = dt("w_f2b", [4, 128, NF * 256], F32, kind="ExternalInput").ap()
    wkv_d = dt("wkv", [2, 128, 4096], F32, kind="ExternalInput").ap()
    w_f3a = dt("w_f3a", [11, 128, 4096], F32, kind="ExternalInput").ap()
    w_f3b = dt("w_f3b", [4, 128, NF * 256], F32, kind="ExternalInput").ap()
    wq_d = dt("wq", [4, 128, 4096], F32, kind="ExternalInput").ap()
    wo1_d = dt("wo1", [4, 64, 16 * 256], F32, kind="ExternalInput").ap()
    w_f4a = dt("w_f4a", [11, 128, 4096], F32, kind="ExternalInput").ap()
    w_f4b = dt("w_f4b", [4, 128, NF * 256], F32, kind="ExternalInput").ap()
    csq_d = dt("csq", [2, 128, TQ], F32, kind="ExternalInput").ap()
    cso_d = dt("cso", [2, 128, T], F32, kind="ExternalInput").ap()
    sinks_d = dt("sinks", [1, 16], F32, kind="ExternalInput").ap()
    mk_d = dt("mk", [3, 128, 512], BF16, kind="ExternalInput").ap()
    cst_d = dt("cst", [6, 128, 512], BF16, kind="ExternalInput").ap()
    outT = dt("outT", [D, T], F32, kind="ExternalOutput").ap()
    h1_all = dt("h1_all", [D, VS], F32).ap()
    q_all = dt("q_all", [D, VS], BF16).ap()
    k_all = dt("k_all", [D, VS], BF16).ap()
    v_all = dt("v_all", [VS, D], BF16).ap()
    oq = dt("oq", [D, TQ], BF16).ap()
    h3q = dt("h3q", [D, TQ], F32).ap()
    kshq = dt("kshq", [256, TQ], BF16).ap()
    vshq = dt("vshq", [TQ, 256], BF16).ap()

    for ses in _stage("1"):
        P = Prog(ctx, ses)
        K = Tok(P, 1024)
        qkst = [P.sb(f"qkst{i}", [128, 512], BF16) for i in range(2)]
        vst = [P.sb(f"vst{i}", [128, D], BF16) for i in range(2)]
        K.load_gains(gains_d, 2)
        K.add_hbuf()
        st = {"qk": 0, "v": 0}
        NCH = VS // 1024
        K.use_hbuf(0)
        K.load_h([(xv[:, 0:1024], 0)])
        for ch in range(NCH):
            cols = slice(ch * 1024, (ch + 1) * 1024)
            if ch + 1 < NCH:
                K.use_hbuf(ch + 1)
                K.load_h([(xv[:, (ch + 1) * 1024:(ch + 2) * 1024], 0)])
            K.use_hbuf(ch)
            K.ffn(0, w_f1a, w_f1b)
            isq = (ch % 2 == 1)
            if isq:
                K.store_h([(h1_all[:, cols], 0)])
            emit_qkv(P, K, 1, wqkv_d, q_all[:, cols], k_all[:, cols], v_all[ch * 1024:(ch + 1) * 1024, :],
                     qkst, vst, st, want_q=isq)
        P.emit()

    for ses in _stage("2a"):
        P = Prog(ctx, ses)
        S_ = SBState(P)
        S_.load_consts(P, cst_d)
        NB = VS // 128
        k_sb = [P.sb(f"k_sb{i}", [128, VS], BF16) for i in range(2)]
        q_sb = [P.sb(f"q_sb{i}", [128, 2, TQ], BF16) for i in range(2)]
        for i in range(2):
            P.memset("pool", q_sb[i][:], 0.0, [("q", i)])
        v_sb = [P.sb(f"v_sb{i}", [128, NB, 128], BF16) for i in range(2)]
        o_sb = [P.sb(f"o_sb{i}", [128, TQ], BF16) for i in range(2)]
        qh_sb = [P.sb(f"qh_sb{i}", [128, 2, 512], BF16) for i in range(2)]
        for i in range(2):
            P.memset("pool", qh_sb[i][:], 0.0, [("q", i)])
        v_view = v_all.rearrange("(n s) f -> s n f", s=128)
        def load_pair(hp):
            sl = hp % 2
            rows = slice(hp * 128, (hp + 1) * 128)
            if DBG_LD & 1:
                P.dma("sp", k_sb[sl][:, :], k_all[rows, :], f"ldk{sl}", (), [("k", sl)], group=hp)
            for j in range(NSEG if DBG_LD & 2 else 0):
                for e in range(2):
                    P.dma("sp", q_sb[sl][e * 64:(e + 1) * 64, e, j * SEGW:(j + 1) * SEGW],
                          q_all[hp * 128 + e * 64:hp * 128 + (e + 1) * 64, seg_start(j):seg_start(j) + SEGW],
                          f"ldk{sl}", (), [("q", sl)], group=hp)
                    P.dma("sp", qh_sb[sl][e * 64:(e + 1) * 64, e, (3 - j) * 128:(4 - j) * 128],
                          q_all[hp * 128 + e * 64:hp * 128 + (e + 1) * 64, seg_start(j):seg_start(j) + 128],
                          f"ldk{sl}", (), [("q", sl)], group=hp)
            for n0 in range(0, NB if DBG_LD & 4 else 0, 8):
                P.dma("sp", v_sb[sl][:, n0:n0 + 8, :], v_view[:, n0:n0 + 8, hp * 128:(hp + 1) * 128],
                      f"ldk{sl}", (), [("v", sl)], group=hp)

        load_pair(0)
        for hp in range(DBG_NHP):
            sl = hp % 2
            rows = slice(hp * 128, (hp + 1) * 128)
            if hp + 1 < DBG_NHP:
                load_pair(hp + 1)
            items = []
            for e in range(2):
                rs = slice(e * 64, e * 64 + 64)
                kq = (("k", sl), ("q", sl), ("v", sl))
                vfn = (lambda kb, sl=sl: v_sb[sl][:, kb, :])
                for j in range(NSEG):
                    vt = 4 * j + 3
                    ocols = slice(j * SEGW + 128, (j + 1) * SEGW)
                    blks = []
                    for m, kb in enumerate(range(4 * vt + 3, -1, -1)):
                        rel = kb - 4 * vt
                        blks.append((kb, 512, 0 if m == 0 else 512, (0, 512, rel) if rel >= 0 else None))

                    def out_own(ot, bkey, sl=sl, rs=rs, j=j, e=e, ocols=ocols):
                        P.copy("dve", o_sb[sl][rs, ocols], P.banks[ot][rs, 0:512], [bkey], [("o", sl, e, j, ocols.start)])
                    if DBG_OWN:
                        items.append(dict(k=k_sb[sl][:, :], q=q_sb[sl][:, e, ocols], v=vfn, rs=rs, blocks=blks, Wmax=512,
                                          out=out_own, keys=kq))
                blks = []
                for m, kb in enumerate(range(16 * 3 + 11, -1, -1)):
                    a = sum(1 for j in range(NSEG) if 16 * j + 11 >= kb)
                    enter = any(16 * j + 11 == kb for j in range(NSEG))
                    W = 128 * a
                    Wold = (W - 128) if enter else W
                    blks.append((kb, W, Wold, (Wold, W, 0) if enter else None))

                def out_halo(ot, bkey, sl=sl, rs=rs, e=e):
                    for j in range(NSEG):
                        P.copy("dve", o_sb[sl][rs, j * SEGW:j * SEGW + 128], P.banks[ot][rs, (3 - j) * 128:(4 - j) * 128],
                               [bkey], [("o", sl, e, j, j * SEGW)])
                if DBG_HALO:
                    items.append(dict(k=k_sb[sl][:, :], q=qh_sb[sl][:, e, :], v=vfn, rs=rs, blocks=blks, Wmax=512,
                                      out=out_halo, keys=kq))
            emit_sb_sweep(P, S_, items)
            okeys = [("o", sl, e, j, c0) for e in range(2) for j in range(NSEG) for c0 in (j * SEGW, j * SEGW + 128)]
            P.dma("sp", oq[rows, :], o_sb[sl][:, :], f"sto{sl}", okeys, [])
        P.emit()

    for ses in _stage("2b"):
        P = Prog(ctx, ses)
        K = Tok(P, 2 * SEGW, TH=2 * SEGW)
        csb = [P.sb(f"csb{i}", [128, 512], F32) for i in range(2)]
        kst = [P.sb(f"kst{i}", [128, 512], BF16) for i in range(2)]
        vst = [P.sb(f"vst{i}", [128, 256], BF16) for i in range(2)]
        for w_ in (2, 3):
            P.dma("sp", K.gains[:, w_, :], gains_d[w_], "const", (), [("gain", w_)], group="c2")
        st = {"k": 0, "v": 0}
        for ch in range(2):
            segs = (2 * ch, 2 * ch + 1)
            K.load_h([(h1_all[:, seg_start(j):seg_start(j) + SEGW], (j % 2) * SEGW) for j in segs])
            qc = slice(ch * 2 * SEGW, (ch + 1) * 2 * SEGW)
            emit_oproj_T(P, K, [(oq[:, qc], 0)], wo0_d)
            K.ffn(2, w_f2a, w_f2b)
            K.store_h([(h3q[:, qc], 0)])
            emit_kv(P, K, 3, wkv_d, [csq_d[0][:, qc], csq_d[1][:, qc]], csb, kst, vst,
                    kshq[:, qc], vshq[ch * 2 * SEGW:(ch + 1) * 2 * SEGW, :], st)
        P.emit()

    for ses in _stage("3"):
        P = Prog(ctx, ses)
        K = Tok(P, T)
        csb = [P.sb(f"csb{i}", [128, 512], F32) for i in range(2)]
        mk_sb = P.sb("mk_sb", [128, 3, 512], BF16)
        sk_sb = P.sb("sk_sb", [65, 16], F32)
        w_sb = [P.sb(f"w_sb{i}", [128, 512], BF16) for i in range(2)]
        dn_sb = P.sb("dn_sb", [65, 512], F32)
        rd_sb = P.sb("rd_sb", [65, 512], F32)
        bc_sb = P.sb("bc_sb", [64, 512], F32)
        for w_ in range(4, 8):
            P.dma("sp", K.gains[:, w_, :], gains_d[w_], "const", (), [("gain", w_)], group="c3")
        for i in range(3):
            P.dma("sp", mk_sb[:, i, :], mk_d[i], "const", (), [("mk", i)], group="c3")
        P.dma("sp", sk_sb[64:65, :], sinks_d[:, :], "const", (), ["sk"], group="c3")
        K.load_h([(h3q[:, j * SEGW + 128:(j + 1) * SEGW], j * 512) for j in range(NSEG)])
        K.ffn(4, w_f3a, w_f3b)
        P.barrier()
        emit_swa_layer(P, K, 5, wq_d, wo1_d, cso_d, csb, kshq, vshq, TQ // 128, [5 * j for j in range(NSEG)],
                       mk_sb, sk_sb, w_sb, dn_sb, rd_sb, bc_sb)
        P.barrier()
        K.ffn(6, w_f4a, w_f4b)
        for half in range(K.NH):
            K.rmsnorm_half(half, 7, final=True)
        K.store_h([(outT, 0)], "sto")
        P.emit()
    return nc


def tile_cols(Wm, ncols_tile=512):
    Kd, N = Wm.shape
    nt = N // ncols_tile
    x = Wm.reshape(8, 128, nt, 2, 256)
    x = x.transpose(2, 1, 3, 0, 4)
    return np.ascontiguousarray(x).reshape(nt, 128, 4096)


def tile_win(w_in):
    g = w_in[:, :DFF].reshape(8, 128, 11, 256)
    u = w_in[:, DFF:].reshape(8, 128, 11, 256)
    x = np.stack([g, u], axis=0)
    x = x.transpose(3, 2, 0, 1, 4)
    return np.ascontiguousarray(x).reshape(11, 128, 4096)


def tile_wout(w_out):
    x = w_out.reshape(NF, 128, 4, 256)
    x = x.transpose(2, 1, 0, 3)
    return np.ascontiguousarray(x).reshape(4, 128, NF * 256)


def gain_layout(g):
    return np.ascontiguousarray(g.reshape(8, 128).T)


def sb_consts():
    import ml_dtypes
    c = np.zeros((6, 128, 512), np.float32)
    s = np.arange(128)[:, None]
    t = np.arange(512)[None, :]
    for r in range(4):
        c[r] = ((r * 128 + s) < t)
    j = np.arange(128)[:, None]
    ss = np.arange(128)[None, :]
    c[4, :, :128] = -(j >= ss).astype(np.float32)
    c[5] = -1.0
    return c.astype(ml_dtypes.bfloat16)


def perm_half(Wm, nheads):
    Kd = Wm.shape[0]
    x = Wm.reshape(Kd, nheads, 2, 32)[:, :, ::-1, :]
    return np.ascontiguousarray(x).reshape(Kd, nheads * 64)


def rope_tables(pos0, Tn):
    half = 32
    inv = (10000.0 ** (-np.arange(half, dtype=np.float32) / half)).astype(np.float32)
    pos = np.arange(pos0, pos0 + Tn, dtype=np.float32)
    ang = pos[None, :] * inv[:, None]
    cos = np.cos(ang).astype(np.float32)
    sin = np.sin(ang).astype(np.float32)
    c64 = np.concatenate([cos, cos], 0)
    s64 = np.concatenate([-sin, sin], 0)
    return np.ascontiguousarray(np.stack([np.concatenate([c64, c64], 0), np.concatenate([s64, s64], 0)]))


def tile_wo8(Wo):
    x = Wo.reshape(8, 128, 4, 256).transpose(2, 1, 0, 3)
    return np.ascontiguousarray(x).reshape(4, 128, 8 * 256)


def tile_pair(Wa, Wb):
    k = Wa.shape[1] // 256
    xa = Wa.reshape(8, 128, k, 256)
    xb = Wb.reshape(8, 128, k, 256)
    x = np.stack([xa, xb], 0).transpose(3, 2, 0, 1, 4)
    return np.ascontiguousarray(x).reshape(k, 128, 4096)


def swa_masks(seq_start):
    import ml_dtypes
    s = np.arange(128)[:, None]
    t = np.tile(np.arange(128), 4)[None, :]
    prev = (s > t).astype(np.float32)
    cur = (s <= t).astype(np.float32)
    first = np.zeros_like(prev) if seq_start else prev
    return np.stack([first, prev, cur]).astype(ml_dtypes.bfloat16)


def kernel(x, ffn1_norm, ffn1_w_in, ffn1_w_out, mix_norm, ffn2_norm, ffn2_w_in, ffn2_w_out,
           sb_w_qkv, sb_w_o, kv_norm, kv_w, swa_w_q, swa_sinks, swa_w_o, final_norm):
    f = lambda a: np.asarray(a, dtype=np.float32)
    x = f(x)
    CPS = NCORES // B
    kvw = f(kv_w)
    Wk, Wv = kvw[:, :256], kvw[:, 256:]
    Wq = f(swa_w_q)[0]
    order = []
    for gp in range(2):
        for j in range(4):
            order += [4 * (2 * gp) + j, 4 * (2 * gp + 1) + j]
    Wq2 = np.ascontiguousarray(Wq.reshape(D, 16, 64)[:, order, :]).reshape(D, D)
    Wo1 = f(swa_w_o)[0]
    shared = {
        "gains": np.stack([gain_layout(g) for g in (f(ffn1_norm)[0], f(mix_norm)[0], f(ffn2_norm)[0], f(kv_norm),
                                                    f(ffn1_norm)[1], f(mix_norm)[1], f(ffn2_norm)[1], f(final_norm))]),
        "w_f1a": tile_win(f(ffn1_w_in)[0]), "w_f1b": tile_wout(f(ffn1_w_out)[0]),
        "wqkv": tile_cols(f(sb_w_qkv)[0]),
        "wo0": tile_wo8(f(sb_w_o)[0]),
        "w_f2a": tile_win(f(ffn2_w_in)[0]), "w_f2b": tile_wout(f(ffn2_w_out)[0]),
        "wkv": np.concatenate([tile_pair(Wk, perm_half(Wk, 4)), tile_pair(Wv, Wv)], 0),
        "w_f3a": tile_win(f(ffn1_w_in)[1]), "w_f3b": tile_wout(f(ffn1_w_out)[1]),
        "wq": tile_pair(Wq2, perm_half(Wq2, 16)),
        "wo1": np.ascontiguousarray(Wo1.reshape(16, 64, 4, 256).transpose(2, 1, 0, 3)).reshape(4, 64, 16 * 256),
        "w_f4a": tile_win(f(ffn2_w_in)[1]), "w_f4b": tile_wout(f(ffn2_w_out)[1]),
        "sinks": f(swa_sinks)[0][None, :],
        "cst": sb_consts(),
    }
    maps = []
    for c in range(NCORES):
        b, r = c // CPS, c % CPS
        shift = (CPS - 1 - r) * 512
        xv = np.zeros((D, VS), np.float32)
        xv[:, shift:] = x[b, :VS - shift].T
        csq = np.concatenate([rope_tables(seg_start(j) - shift, SEGW) for j in range(NSEG)], axis=2)
        cso = np.concatenate([rope_tables(seg_start(j) + 128 - shift, 512) for j in range(NSEG)], axis=2)
        maps.append(dict(shared, xv=xv, csq=np.ascontiguousarray(csq), cso=np.ascontiguousarray(cso),
                         mk=swa_masks(r == 0)))
    es = ExitStack()
    nc = build_fused(es)
    res = run_bass_kernel_spmd(nc, maps, core_ids=list(range(NCORES)))
    es.close()
    out = np.empty((B, S, D), np.float32)
    for c in range(NCORES):
        b, r = c // CPS, c % CPS
        oT = res.results[c]["outT"]
        for j in range(NSEG):
            t0 = (4 * j + r) * 512
            out[b, t0:t0 + 512, :] = oT[:, j * 512:(j + 1) * 512].T
    return out
```

```python
import numpy as np
from contextlib import ExitStack
import concourse.bass as bass
import concourse.mybir as mybir
from concourse.bass_utils import run_bass_kernel_spmd

F32 = mybir.dt.float32
BF16 = mybir.dt.bfloat16
AF = mybir.ActivationFunctionType
ALU = mybir.AluOpType

D = 1024
DFF = 2816
NF = DFF // 128
S = 8192
B = 2
NCORES = 8
T = 2048
EPS = 1e-6
SAME_ENGINE_SYNC = False
RENG = "dve"
RENG2 = "dve"


class Op:
    __slots__ = ("eng", "fn", "deps", "needed", "sem", "val", "inc", "group")

    def __init__(self, eng, fn, sem, inc, group=None):
        self.eng, self.fn, self.sem, self.inc, self.group = eng, fn, sem, inc, group
        self.deps = []
        self.needed = False
        self.val = 0


class Ctx:
    def __init__(self, nc, es):
        self.nc, self.es = nc, es
        self.sems = {}
        self.totals = {}
        self.banks = [es.enter_context(nc.psum_tensor(f"bank{i}", [128, 512], F32)) for i in range(8)]
        self.nstage = 0


class Prog:
    ENGS = ("pe", "act", "dve", "pool", "sp")

    def __init__(self, ctx, es):
        self.ctx = ctx
        self.nc, self.es = ctx.nc, es
        self.ops = []
        self.last_w = {}
        self.readers = {}
        self.bank_i = 0
        self.banks = ctx.banks
        self.barrier_ops = []
        ctx.nstage += 1
        self.tag = f"g{ctx.nstage}_"

    def sb(self, name, shape, dt):
        return self.es.enter_context(self.nc.sbuf_tensor(self.tag + name, list(shape), dt))

    def bank(self):
        i = self.bank_i
        self.bank_i = (self.bank_i + 1) % len(self.banks)
        return i

    def add(self, eng, fn, reads=(), writes=(), sem=None, inc=1, group=None):
        op = Op(eng, fn, sem if sem is not None else eng, inc, group)
        deps = {}
        for k in reads:
            w = self.last_w.get(k)
            if w is not None:
                deps[id(w)] = w
        for k in writes:
            w = self.last_w.get(k)
            if w is not None:
                deps[id(w)] = w
            for r in self.readers.get(k, ()):
                deps[id(r)] = r
        for b in self.barrier_ops:
            deps[id(b)] = b
        op.deps = [d for d in deps.values()
                   if not (group is not None and d.group == group and d.sem == op.sem)]
        for k in reads:
            self.readers.setdefault(k, []).append(op)
        for k in writes:
            self.last_w[k] = op
            self.readers[k] = []
        self.ops.append(op)
        return op

    def barrier(self):
        last = {}
        for op in self.ops:
            last[(op.eng, op.sem)] = op
        self.barrier_ops = list(last.values())
        self.last_w = {}
        self.readers = {}

    def mm(self, out, lhsT, rhs, start, stop, reads, writes):
        return self.add("pe", lambda e: e.matmul(out, lhsT=lhsT, rhs=rhs, start=start, stop=stop),
                        reads, writes)

    def act(self, out, in_, func, reads, writes, bias=0.0, scale=1.0):
        return self.add("act", lambda e: e.activation(out=out, in_=in_, func=func, bias=bias, scale=scale),
                        reads, writes)

    def tt(self, eng, out, in0, in1, op, reads, writes):
        return self.add(eng, lambda e: e.tensor_tensor(out=out, in0=in0, in1=in1, op=op), reads, writes)

    def stt(self, eng, out, in0, scalar, in1, op0, op1, reads, writes):
        return self.add(eng, lambda e: e.scalar_tensor_tensor(out=out, in0=in0, scalar=scalar, in1=in1,
                                                              op0=op0, op1=op1), reads, writes)

    def ts(self, eng, out, in0, s1, s2, op0, op1, reads, writes):
        return self.add(eng, lambda e: e.tensor_scalar(out=out, in0=in0, scalar1=s1, scalar2=s2,
                                                       op0=op0, op1=op1), reads, writes)

    def copy(self, eng, out, in_, reads, writes):
        if eng == "act":
            return self.add(eng, lambda e: e.copy(out=out, in_=in_), reads, writes)
        return self.add(eng, lambda e: e.tensor_copy(out=out, in_=in_), reads, writes)

    def recip(self, out, in_, reads, writes):
        return self.add("dve", lambda e: e.reciprocal(out=out, in_=in_), reads, writes)

    def memset(self, eng, ap, val, writes):
        return self.add(eng, lambda e: e.memset(ap, val), (), writes)

    def dma(self, q, out, in_, sem, reads, writes, group=None):
        return self.add(q, lambda e: e.dma_start(out=out, in_=in_), reads, writes, sem=sem, inc=16, group=group)

    def emit(self):
        nc, ctx = self.nc, self.ctx
        for op in self.ops:
            for d in op.deps:
                d.needed = True
        last_of = {}
        for op in self.ops:
            last_of[op.eng] = op
        for op in last_of.values():
            op.needed = True
        counts = dict(ctx.totals)
        for op in self.ops:
            if op.inc == 16:
                op.needed = True
            if op.needed:
                counts[op.sem] = counts.get(op.sem, 0) + op.inc
                op.val = counts[op.sem]
        gmax = {}
        for op in self.ops:
            if op.group is not None:
                k = (op.sem, op.group)
                gmax[k] = max(gmax.get(k, 0), op.val)
        for op in self.ops:
            if op.group is not None:
                op.val = gmax[(op.sem, op.group)]
        for name in counts:
            if name not in ctx.sems:
                ctx.sems[name] = ctx.es.enter_context(nc.semaphore("s_" + name))
        ctx.totals = counts
        sems, totals = ctx.sems, counts
        by_eng = {e: [o for o in self.ops if o.eng == e] for e in self.ENGS}

        def run(engname, e):
            waited = {}
            for op in by_eng[engname]:
                need = {}
                for d in op.deps:
                    if d.eng == engname and d.sem == engname:
                        if engname == "pe" or not SAME_ENGINE_SYNC:
                            continue
                    if need.get(d.sem, 0) < d.val:
                        need[d.sem] = d.val
                for sname, v in need.items():
                    if waited.get(sname, 0) < v:
                        e.wait_ge(sems[sname], v)
                        waited[sname] = v
                ins = op.fn(e)
                if op.needed:
                    ins.then_inc(sems[op.sem], op.inc)
            for sname, v in totals.items():
                if v > 0 and sname != engname:
                    e.wait_ge(sems[sname], v)

        block = self.es.enter_context(nc.Block())

        @block.tensor
        def _(e):
            run("pe", e)

        @block.scalar
        def _(e):
            run("act", e)

        @block.vector
        def _(e):
            run("dve", e)

        @block.gpsimd
        def _(e):
            run("pool", e)

        @block.sync
        def _(e):
            run("sp", e)


class Tok:
    def __init__(self, P, Tn, TH=None):
        self.P = P
        self.T = Tn
        self.TH = TH if TH is not None else min(Tn, 1024)
        self.NH = Tn // self.TH
        self.tiles = [(s0, min(512, self.TH - s0)) for s0 in range(0, self.TH, 512)]
        self.NTT = len(self.tiles)
        self.hT = P.sb("hT", [128, 8, Tn], F32)
        self.hname = "hT"
        self.hbufs = [(self.hT, "hT")]
        self.xn = P.sb("xn", [128, 8, self.TH], BF16)
        self.hff = P.sb("hff", [128, NF, self.TH], BF16)
        self.win = [P.sb(f"win_sb{i}", [128, 2, 8, 256], BF16) for i in range(2)]
        self.wout = [P.sb(f"wout_sb{i}", [128, NF, 256], BF16) for i in range(2)]
        self.sq = [P.sb(f"sq{i}", [128, 512], F32) for i in range(2)]
        self.sg = [P.sb(f"sg{i}", [128, 512], F32) for i in range(2)]
        self.sd = P.sb("sd", [128, 512], F32)
        self.rstd = P.sb("rstd", [128, 512], F32)
        self.ones = P.sb("ones", [128, 128], F32)
        self.gains = P.sb("gains_sb", [128, 8, 8], F32)
        self.win_i = 0
        self.wout_i = 0
        self.sq_i = 0
        self.sg_i = 0
        self.uid = 0
        P.memset("dve", self.ones[:], 1.0, ["ones"])

    def add_hbuf(self):
        t = self.P.sb("hT2", [128, 8, self.T], F32)
        self.hbufs.append((t, "hT2"))

    def use_hbuf(self, i):
        self.hT, self.hname = self.hbufs[i % len(self.hbufs)]

    def win_slot(self):
        i = self.win_i
        self.win_i ^= 1
        return i

    def wout_slot(self):
        i = self.wout_i
        self.wout_i ^= 1
        return i

    def gid(self):
        self.uid += 1
        return self.uid

    def all_tiles(self):
        return [(h, tt) for h in range(self.NH) for tt in range(self.NTT)]

    def load_gains(self, gains_d, n):
        for w in range(n):
            self.P.dma("sp", self.gains[:, w, :], gains_d[w], "const", (), [("gain", w)], group="c")

    def load_h(self, srcs):
        g = self.gid()
        keys = [(self.hname, c, gt) for c in range(8) for gt in range(self.NH * self.NTT)]
        for (src, off) in srcs:
            n = src.shape[1]
            for c in range(8):
                self.P.dma("sp", self.hT[:, c, off:off + n], src[c * 128:(c + 1) * 128, :], "ld" + self.hname, (),
                           [(self.hname, c, gt) for gt in range(self.NH * self.NTT)], group=g)

    def store_h(self, dsts, sem="sth"):
        g = self.gid()
        for (dst, off) in dsts:
            n = dst.shape[1]
            for c in range(8):
                self.P.dma("sp", dst[c * 128:(c + 1) * 128, :], self.hT[:, c, off:off + n], sem,
                           [(self.hname, c, gt) for gt in range(self.NH * self.NTT)], [], group=g)

    def rmsnorm_half(self, half, which, final=False):
        P = self.P
        for tt, (t0, tw) in enumerate(self.tiles):
            gt = half * self.NTT + tt
            gs = slice(half * self.TH + t0, half * self.TH + t0 + tw)
            ls = slice(t0, t0 + tw)
            b = P.bank()
            bk = ("bank", b)
            for c in range(8):
                j = self.sq_i
                self.sq_i ^= 1
                sq = self.sq[j]
                P.act(sq[:, 0:tw], self.hT[:, c, gs], AF.Square, [(self.hname, c, gt)], [("sq", j)])
                P.mm(P.banks[b][:, 0:tw], self.ones[:], sq[:, 0:tw], c == 0, c == 7, [("sq", j), "ones"], [bk])
            P.act(self.sd[:, 0:tw], P.banks[b][:, 0:tw], AF.Sqrt, [bk], ["sd"], bias=EPS, scale=1.0 / D)
            P.recip(self.rstd[:, 0:tw], self.sd[:, 0:tw], ["sd"], ["rstd"])
            for c in range(8):
                if final:
                    P.stt("dve", self.hT[:, c, gs], self.hT[:, c, gs], self.gains[:, which, c:c + 1],
                          self.rstd[:, 0:tw], ALU.mult, ALU.mult,
                          [(self.hname, c, gt), "rstd", ("gain", which)], [(self.hname, c, gt)])
                else:
                    P.stt("dve", self.xn[:, c, ls], self.hT[:, c, gs], self.gains[:, which, c:c + 1],
                          self.rstd[:, 0:tw], ALU.mult, ALU.mult,
                          [(self.hname, c, gt), "rstd", ("gain", which)], [("xn", c, tt)])

    def ffn(self, which_gain, win_d, wout_d):
        P = self.P
        for half in range(self.NH):
            self.rmsnorm_half(half, which_gain)
            for g in range(11):
                s = self.win_slot()
                w = self.win[s]
                P.dma("pool", w[:].rearrange("p a k n -> p (a k n)"), win_d[g], f"win{s}", (), [("win", s)])
                for a in range(2):
                    f = g * 2 + a
                    cs = slice(a * 128, (a + 1) * 128)
                    for tt, (t0, tw) in enumerate(self.tiles):
                        ls = slice(t0, t0 + tw)
                        bg, bu = P.bank(), P.bank()
                        for kc in range(8):
                            P.mm(P.banks[bg][:, 0:tw], w[:, 0, kc, cs], self.xn[:, kc, ls], kc == 0, kc == 7,
                                 [("win", s), ("xn", kc, tt)], [("bank", bg)])
                        for kc in range(8):
                            P.mm(P.banks[bu][:, 0:tw], w[:, 1, kc, cs], self.xn[:, kc, ls], kc == 0, kc == 7,
                                 [("win", s), ("xn", kc, tt)], [("bank", bu)])
                        j = self.sg_i
                        self.sg_i ^= 1
                        P.act(self.sg[j][:, 0:tw], P.banks[bg][:, 0:tw], AF.Silu, [("bank", bg)], [("sg", j)])
                        P.tt("dve", self.hff[:, f, ls], self.sg[j][:, 0:tw], P.banks[bu][:, 0:tw], ALU.mult,
                             [("sg", j), ("bank", bu)], [("hff", f, tt)])
            for dcp in range(4):
                s = self.wout_slot()
                w = self.wout[s]
                P.dma("pool", w[:].rearrange("p f n -> p (f n)"), wout_d[dcp], f"wout{s}", (), [("wout", s)])
                for a in range(2):
                    dc = dcp * 2 + a
                    cs = slice(a * 128, (a + 1) * 128)
                    for tt, (t0, tw) in enumerate(self.tiles):
                        gt = half * self.NTT + tt
                        gs = slice(half * self.TH + t0, half * self.TH + t0 + tw)
                        ls = slice(t0, t0 + tw)
                        b = P.bank()
                        for f in range(NF):
                            P.mm(P.banks[b][:, 0:tw], w[:, f, cs], self.hff[:, f, ls], f == 0, f == NF - 1,
                                 [("wout", s), ("hff", f, tt)], [("bank", b)])
                        P.stt("dve", self.hT[:, dc, gs], P.banks[b][:, 0:tw], 0.5, self.hT[:, dc, gs],
                              ALU.mult, ALU.add, [("bank", b), (self.hname, dc, gt)], [(self.hname, dc, gt)])


def emit_qkv(P, K, gain_i, wqkv_d, q_dst, k_dst, v_dst, qkst, vst, st, want_q=True):
    K.rmsnorm_half(0, gain_i)
    for i in range(4):
        if i < 2 and not want_q:
            continue
        s = K.win_slot()
        w = K.win[s]
        P.dma("pool", w[:].rearrange("p a k n -> p (a k n)"), wqkv_d[i], f"win{s}", (), [("win", s)])
        for a in range(2):
            for h2 in range(2):
                chunk = (i % 2) * 4 + a * 2 + h2
                cs = slice(h2 * 128, (h2 + 1) * 128)
                for tt, (t0, tw) in enumerate(K.tiles):
                    ls = slice(t0, t0 + tw)
                    b = P.bank()
                    for kc in range(8):
                        P.mm(P.banks[b][:, 0:tw], w[:, a, kc, cs], K.xn[:, kc, ls], kc == 0, kc == 7,
                             [("win", s), ("xn", kc, tt)], [("bank", b)])
                    j = st["qk"]
                    st["qk"] ^= 1
                    dst = q_dst if i < 2 else k_dst
                    P.act(qkst[j][:, 0:tw], P.banks[b][:, 0:tw], AF.Copy, [("bank", b)], [("qkst", j)],
                          scale=(0.125 if i < 2 else 1.0))
                    P.dma("sp", dst[chunk * 128:(chunk + 1) * 128, t0:t0 + tw], qkst[j][:, 0:tw],
                          f"stqk{j}", [("qkst", j)], [])
    slots = []
    for i in (4, 5):
        s = K.win_slot()
        P.dma("pool", K.win[s][:].rearrange("p a k n -> p (a k n)"), wqkv_d[i], f"win{s}", (), [("win", s)])
        slots.append(s)
    for tb in range(K.TH // 128):
        j = st["v"]
        st["v"] ^= 1
        for cb in range(4):
            s = slots[cb // 2]
            a = cb % 2
            b = P.bank()
            for kc in range(8):
                P.mm(P.banks[b][:, 0:256], K.xn[:, kc, tb * 128:(tb + 1) * 128], K.win[s][:, a, kc, :],
                     kc == 0, kc == 7, [("win", s), ("xn", kc, tb // 4)], [("bank", b)])
            P.copy("dve", vst[j][:, cb * 256:(cb + 1) * 256], P.banks[b][:, 0:256], [("bank", b)], [("vst", j, cb)])
        P.dma("sp", v_dst[tb * 128:(tb + 1) * 128, :], vst[j][:], f"stv{j}",
              [("vst", j, cb) for cb in range(4)], [])


def emit_oproj_T(P, K, o_srcs, wo_d):
    g = K.gid()
    for (src, off) in o_srcs:
        n = src.shape[1]
        for fc in range(8):
            P.dma("sp", K.xn[:, fc, off:off + n], src[fc * 128:(fc + 1) * 128, :],
                  "ldx", (), [("xn", fc, tt) for tt in range(K.NTT)], group=g)
    for dcp in range(4):
        s = K.wout_slot()
        w = K.wout[s]
        P.dma("pool", w[:, 0:8, :], wo_d[dcp].rearrange("p (f n) -> p f n", f=8), f"wout{s}", (), [("wout", s)])
        for a in range(2):
            dc = dcp * 2 + a
            cs = slice(a * 128, (a + 1) * 128)
            for tt, (t0, tw) in enumerate(K.tiles):
                ls = slice(t0, t0 + tw)
                b = P.bank()
                for fc in range(8):
                    P.mm(P.banks[b][:, 0:tw], w[:, fc, cs], K.xn[:, fc, ls], fc == 0, fc == 7,
                         [("wout", s), ("xn", fc, tt)], [("bank", b)])
                P.tt("dve", K.hT[:, dc, ls], P.banks[b][:, 0:tw], K.hT[:, dc, ls], ALU.add,
                     [("bank", b), (K.hname, dc, tt)], [(K.hname, dc, tt)])


def emit_rot_proj(P, K, half, w, s, a_main, a_perm, ccols, cs_d, csb, dst_fn):
    for tt, (t0, tw) in enumerate(K.tiles):
        gt = half * K.NTT + tt
        gs = slice(half * K.TH + t0, half * K.TH + t0 + tw)
        ls = slice(t0, t0 + tw)
        b1, b2 = P.bank(), P.bank()
        for kc in range(8):
            P.mm(P.banks[b1][:, 0:tw], w[:, a_main, kc, ccols], K.xn[:, kc, ls], kc == 0, kc == 7,
                 [("win", s), ("xn", kc, tt)], [("bank", b1)])
        for kc in range(8):
            P.mm(P.banks[b2][:, 0:tw], w[:, a_perm, kc, ccols], K.xn[:, kc, ls], kc == 0, kc == 7,
                 [("win", s), ("xn", kc, tt)], [("bank", b2)])
        P.dma("sp", csb[0][:, 0:tw], cs_d[0][:, gs], "cs0", (), ["cs0"])
        P.dma("sp", csb[1][:, 0:tw], cs_d[1][:, gs], "cs1", (), ["cs1"])
        P.tt("dve", K.sq[0][:, 0:tw], P.banks[b1][:, 0:tw], csb[0][:, 0:tw], ALU.mult, [("bank", b1), "cs0"], [("sq", 0)])
        P.tt("dve", K.sq[1][:, 0:tw], P.banks[b2][:, 0:tw], csb[1][:, 0:tw], ALU.mult, [("bank", b2), "cs1"], [("sq", 1)])
        dst_fn(gt, gs, tw)


def emit_kv(P, K, gain_i, wkv_d, cs_d, csb, kst, vst, ksh, vsh, st):
    K.rmsnorm_half(0, gain_i)
    s = K.win_slot()
    w = K.win[s]
    P.dma("pool", w[:].rearrange("p a k n -> p (a k n)"), wkv_d[0], f"win{s}", (), [("win", s)])
    for c in range(2):
        def dst(gt, gs, tw, c=c):
            j = st["k"]
            st["k"] ^= 1
            P.tt("dve", kst[j][:, 0:tw], K.sq[0][:, 0:tw], K.sq[1][:, 0:tw], ALU.add, [("sq", 0), ("sq", 1)], [("kst", j)])
            P.dma("sp", ksh[c * 128:(c + 1) * 128, gs], kst[j][:, 0:tw], f"stk{j}", [("kst", j)], [])
        emit_rot_proj(P, K, 0, w, s, 0, 1, slice(c * 128, (c + 1) * 128), cs_d, csb, dst)
    s = K.win_slot()
    w = K.win[s]
    P.dma("pool", w[:].rearrange("p a k n -> p (a k n)"), wkv_d[1], f"win{s}", (), [("win", s)])
    for tb in range(K.TH // 128):
        j = st["v"]
        st["v"] ^= 1
        b = P.bank()
        for kc in range(8):
            P.mm(P.banks[b][:, 0:256], K.xn[:, kc, tb * 128:(tb + 1) * 128], w[:, 0, kc, :],
                 kc == 0, kc == 7, [("win", s), ("xn", kc, tb // 4)], [("bank", b)])
        P.copy("dve", vst[j][:, 0:256], P.banks[b][:, 0:256], [("bank", b)], [("vst", j)])
        P.dma("sp", vsh[tb * 128:(tb + 1) * 128, :], vst[j][:, 0:256], f"stv{j}", [("vst", j)], [])


class SBState:
    def __init__(self, P):
        self.c_sb = P.sb("c_sb", [128, 6, 512], BF16)
        self.e_sb = [P.sb(f"e_sb{i}", [128, 512], F32) for i in range(2)]
        self.sp_sb = [P.sb(f"sp_sb{i}", [128, 512], BF16) for i in range(4)]
        self.w_sb = [P.sb(f"w_sb{i}", [128, 512], BF16) for i in range(2)]
        self.R_sb = P.sb("R_sb", [128, 512], F32)
        self.Rb_sb = [P.sb(f"Rb_sb{i}", [128, 512], BF16) for i in range(2)]
        self.nblk = 0
        self.nitem = 0

    def load_consts(self, P, cst):
        for i in range(6):
            P.dma("sp", self.c_sb[:, i, :], cst[i], "const", (), [("c", i)], group="c")


def emit_sb_sweep(P, S_, items):
    c_sb, e_sb, sp_sb, w_sb, R_sb, Rb_sb = S_.c_sb, S_.e_sb, S_.sp_sb, S_.w_sb, S_.R_sb, S_.Rb_sb
    negtri = c_sb[:, 4, 0:128]
    negones = c_sb[:, 5, 0:128]
    blocks = []
    for t, it in enumerate(items):
        nb = len(it["blocks"])
        for m, (kb, W, Wold, mask) in enumerate(it["blocks"]):
            blocks.append((it, kb, W, Wold, mask, m == nb - 1, m, S_.nitem + t))
    S_.nitem += len(items)
    n = len(blocks)
    if n == 0:
        return
    base = S_.nblk
    S_.nblk += n

    def A(i):
        it, kb, W, Wold, mask, last, m, t = blocks[i]
        g = base + i
        za = g % 2
        kk, qk, vk = it["keys"]
        P.mm(P.banks[za][:, 0:W], it["k"][:, kb * 128:(kb + 1) * 128], it["q"][:, 0:W], True, True,
             [kk, qk], [("bank", za)])

    def E(i):
        it, kb, W, Wold, mask, last, m, t = blocks[i]
        g = base + i
        P.act(e_sb[g % 2][:, 0:W], P.banks[g % 2][:, 0:W], AF.Exp, [("bank", g % 2)], [("e", g % 2)])

    def L(i):
        it, kb, W, Wold, mask, last, m, t = blocks[i]
        g = base + i
        sj = g % 4
        P.act(sp_sb[sj][:, 0:W], e_sb[g % 2][:, 0:W], AF.Ln, [("e", g % 2)], [("sp", sj)], bias=1.0)
        if mask is not None:
            c0, c1, rel = mask
            P.tt("dve", sp_sb[sj][:, c0:c1], sp_sb[sj][:, c0:c1], c_sb[:, rel, 0:c1 - c0], ALU.mult,
                 [("sp", sj), ("c", rel)], [("sp", sj)])

    def ZB(i):
        it, kb, W, Wold, mask, last, m, t = blocks[i]
        g = base + i
        sj = g % 4
        zb = 2 + g % 2
        kk, qk, vk = it["keys"]
        P.mm(P.banks[zb][:, 0:W], it["k"][:, kb * 128:(kb + 1) * 128], it["q"][:, 0:W], True, False,
             [kk, qk], [("bank", zb)])
        P.mm(P.banks[zb][:, 0:W], negtri, sp_sb[sj][:, 0:W], False, Wold == 0, [("sp", sj), ("c", 4)], [("bank", zb)])
        if Wold > 0:
            rb = (m - 1) % 2
            P.mm(P.banks[zb][:, 0:Wold], negones, Rb_sb[rb][:, 0:Wold], False, True,
                 [("rb", rb), ("c", 5)], [("bank", zb)])

    def Wx(i):
        it, kb, W, Wold, mask, last, m, t = blocks[i]
        g = base + i
        zb = 2 + g % 2
        wj = g % 2
        P.act(w_sb[wj][:, 0:W], P.banks[zb][:, 0:W], AF.Exp, [("bank", zb)], [("w", wj)])
        if mask is not None:
            c0, c1, rel = mask
            P.tt("dve", w_sb[wj][:, c0:c1], w_sb[wj][:, c0:c1], c_sb[:, rel, 0:c1 - c0], ALU.mult,
                 [("w", wj), ("c", rel)], [("w", wj)])
        if Wold == 0 and W < it["Wmax"]:
            P.memset("pool", w_sb[wj][:, W:it["Wmax"]], 0.0, [("w", wj)])

    def PV(i):
        it, kb, W, Wold, mask, last, m, t = blocks[i]
        g = base + i
        kk, qk, vk = it["keys"]
        ot = 4 + t % 2
        Wm = it["Wmax"]
        if Wold == 0:
            P.mm(P.banks[ot][:, 0:Wm], it["v"](kb), w_sb[g % 2][:, 0:Wm], True, last,
                 [vk, ("w", g % 2)], [("bank", ot)])
        else:
            P.mm(P.banks[ot][:, 0:W], it["v"](kb), w_sb[g % 2][:, 0:W], False, last,
                 [vk, ("w", g % 2)], [("bank", ot)])
        if last:
            it["out"](ot, ("bank", ot))

    def Rx(i):
        it, kb, W, Wold, mask, last, m, t = blocks[i]
        g = base + i
        sj = g % 4
        if last:
            return
        if W > Wold:
            P.copy(RENG, R_sb[:, Wold:W], sp_sb[sj][:, Wold:W], [("sp", sj)], ["R"])
        if Wold > 0:
            P.tt(RENG, R_sb[:, 0:Wold], R_sb[:, 0:Wold], sp_sb[sj][:, 0:Wold], ALU.add, [("sp", sj), "R"], ["R"])
            P.copy(RENG2, Rb_sb[m % 2][:, 0:W], R_sb[:, 0:W], ["R"], [("rb", m % 2)])
        else:
            P.copy(RENG2, Rb_sb[m % 2][:, 0:W], sp_sb[sj][:, 0:W], [("sp", sj)], [("rb", m % 2)])

    A(0)
    for k in range(n + 3):
        if 0 <= k - 2 < n:
            ZB(k - 2)
        if 0 <= k - 3 < n:
            PV(k - 3)
        if k + 1 < n:
            A(k + 1)
        if k < n:
            E(k)
            L(k)
        if 0 <= k - 2 < n:
            Wx(k - 2)
        if 0 <= k - 1 < n:
            Rx(k - 1)


def emit_swa_layer(P, K, gain_i, wq_d, wo_d, cs_d, csb, kh, vh, NBK, kbase, mk_sb, sk_sb, w_sb, dn_sb, rd_sb, bc_sb):
    Tn = K.T
    KW = NBK * 128
    q_sb = K.hff[:].rearrange("p f t -> p (f t)")[:, 0:8 * Tn].rearrange("p (c t) -> p c t", c=8)
    k_sb = K.wout[0][:].rearrange("p f n -> p (f n)")[:, 0:2 * KW].rearrange("p (c t) -> p c t", c=2)
    v_sb = K.wout[1][:].rearrange("p f n -> p (f n)")[:, 0:NBK * 4 * 65].rearrange("p (b g d) -> p b g d", b=NBK, g=4)
    o_sb = K.xn[:].rearrange("p c t -> p (c t)")[:, 0:16 * 512].rearrange("p (h t) -> p h t", h=16)

    for half in range(K.NH):
        K.rmsnorm_half(half, gain_i)
        for i in range(4):
            s = K.win_slot()
            w = K.win[s]
            P.dma("pool", w[:].rearrange("p a k n -> p (a k n)"), wq_d[i], f"win{s}", (), [("win", s)])
            for c2 in range(2):
                chunk = 2 * i + c2

                def dst(gt, gs, tw, chunk=chunk):
                    P.tt("dve", q_sb[:, chunk, gs], K.sq[0][:, 0:tw], K.sq[1][:, 0:tw], ALU.add,
                         [("sq", 0), ("sq", 1)], [("q", chunk, gt)])
                emit_rot_proj(P, K, half, w, s, 0, 1, slice(c2 * 128, (c2 + 1) * 128), cs_d, csb, dst)
    P.barrier()
    for c in range(2):
        P.dma("sp", k_sb[:, c, :], kh[c * 128:(c + 1) * 128, :], "ldkv", (), [("k", c)], group="kv")
    P.memset("dve", K.wout[1][:], 1.0, ["vones"])
    vh_v = vh.rearrange("(b s) (g d) -> s b g d", s=128, g=4)
    for g in range(4):
        P.dma("sp", v_sb[:, :, g, 0:64], vh_v[:, :, g, :], "ldkv", ["vones"], [("v", g)], group="kv")
    P.act(sk_sb[64:65, :], sk_sb[64:65, :], AF.Exp, ["sk"], ["sk"])
    st = {"w": 0}
    for tt in range(Tn // 512):
        gsl = slice(tt * 512, (tt + 1) * 512)
        units = [(ib, g) for ib in range(4) for g in range(4)]
        pairs_ = [(u, kbi) for u in range(len(units)) for kbi in range(2)]
        sbank = {}
        obank = {}
        wbuf = {}

        def S(p):
            u, kbi = pairs_[p]
            ib, g = units[u]
            i = tt * 4 + ib
            gp, e = g // 2, g % 2
            rs = slice(e * 64, e * 64 + 64)
            kb = kbase[tt] + ib + kbi
            sbk = P.bank()
            sbank[p] = sbk
            for j in range(4):
                P.mm(P.banks[sbk][:, j * 128:(j + 1) * 128], k_sb[rs, gp, kb * 128:(kb + 1) * 128],
                     q_sb[rs, gp * 4 + j, i * 128:(i + 1) * 128], True, True,
                     [("k", gp), ("q", gp * 4 + j, tt)], [("bank", sbk)])

        def X(p):
            u, kbi = pairs_[p]
            ib, g = units[u]
            i = tt * 4 + ib
            sbk = sbank[p]
            wj = st["w"]
            st["w"] ^= 1
            wbuf[p] = wj
            P.act(w_sb[wj][:], P.banks[sbk][:], AF.Exp, [("bank", sbk)], [("w", wj)], scale=0.125)
            mi = 2 if kbi == 1 else (0 if i == 0 else 1)
            P.tt("dve", w_sb[wj][:], w_sb[wj][:], mk_sb[:, mi, :], ALU.mult, [("w", wj), ("mk", mi)], [("w", wj)])

        def V(p):
            u, kbi = pairs_[p]
            ib, g = units[u]
            kb = kbase[tt] + ib + kbi
            if kbi == 0:
                obank[u] = P.bank()
            ob = obank[u]
            wj = wbuf[p]
            P.mm(P.banks[ob][0:65, :], v_sb[:, kb, g, :], w_sb[wj][:], kbi == 0, kbi == 1,
                 [("v", g), ("w", wj)], [("bank", ob)])

        def N(u):
            ib, g = units[u]
            ob = obank[u]
            P.tt("dve", dn_sb[64:65, :].rearrange("p (j t) -> p j t", j=4),
                 P.banks[ob][64:65, :].rearrange("p (j t) -> p j t", j=4),
                 sk_sb[64:65, 4 * g:4 * g + 4].unsqueeze(2).to_broadcast([1, 4, 128]), ALU.add,
                 [("bank", ob), "sk"], ["dn"])
            P.act(rd_sb[64:65, :], dn_sb[64:65, :], AF.Ln, ["dn"], ["rd"])
            P.act(rd_sb[64:65, :], rd_sb[64:65, :], AF.Exp, ["rd"], ["rd"], scale=-1.0)
            bb = P.bank()
            P.mm(P.banks[bb][0:64, :], K.ones[64:65, 0:64], rd_sb[64:65, :], True, True, ["rd", "ones"], [("bank", bb)])
            P.copy("act", bc_sb[:, :], P.banks[bb][0:64, :], [("bank", bb)], ["bc"])
            P.tt("dve", o_sb[0:64, 4 * g:4 * g + 4, ib * 128:(ib + 1) * 128],
                 P.banks[ob][0:64, :].rearrange("p (j t) -> p j t", j=4),
                 bc_sb[:, :].rearrange("p (j t) -> p j t", j=4), ALU.mult,
                 [("bank", ob), "bc"], [("o", g, ib)])

        npair = len(pairs_)
        S(0)
        S(1)
        for p in range(npair):
            if p + 2 < npair:
                S(p + 2)
            X(p)
            V(p)
            if pairs_[p][1] == 1:
                N(pairs_[p][0])
        for dcp in range(4):
            s = K.win_slot()
            wv = K.win[s][:].rearrange("p a k n -> p (a k n)")[0:64, :].rearrange("p (h n) -> p h n", h=16)
            P.dma("pool", K.win[s][:].rearrange("p a k n -> p (a k n)")[0:64, :], wo_d[dcp], f"win{s}", (), [("win", s)])
            for a in range(2):
                dc = dcp * 2 + a
                b = P.bank()
                for h in range(16):
                    P.mm(P.banks[b][:], wv[:, h, a * 128:(a + 1) * 128], o_sb[0:64, h, :], h == 0, h == 15,
                         [("win", s)] + [("o", h // 4, ib) for ib in range(4)], [("bank", b)])
                P.tt("dve", K.hT[:, dc, gsl], P.banks[b][:], K.hT[:, dc, gsl], ALU.add,
                     [("bank", b), (K.hname, dc, tt)], [(K.hname, dc, tt)])


VS = 8192
NSEG = 4
SEGW = 640
TQ = NSEG * SEGW


def seg_start(j):
    return (4 * j + 3) * 512 - 128


STAGES = ("1", "2a", "2b", "3")
DBG_NHP = 8
DBG_HALO = True
DBG_OWN = True
DBG_LD = 7


def _stage(name):
    if name in STAGES:
        with ExitStack() as ses:
            yield ses


def build_fused(es):
    nc = bass.Bass("TRN2", target_bir_lowering=False)
    ctx = Ctx(nc, es)
    dt = nc.dram_tensor
    xv = dt("xv", [D, VS], F32, kind="ExternalInput").ap()
    gains_d = dt("gains", [8, 128, 8], F32, kind="ExternalInput").ap()
    w_f1a = dt("w_f1a", [11, 128, 4096], F32, kind="ExternalInput").ap()
    w_f1b = dt("w_f1b", [4, 128, NF * 256], F32, kind="ExternalInput").ap()
    wqkv_d = dt("wqkv", [6, 128, 4096], F32, kind="ExternalInput").ap()
    wo0_d = dt("wo0", [4, 128, 8 * 256], F32, kind="ExternalInput").ap()
    w_f2a = dt("w_f2a", [11, 128, 4096], F32, kind="ExternalInput").ap()
    w_f2b = dt("w_f2b", [4, 128, NF * 256], F32, kind="ExternalInput").ap()
    wkv_d = dt("wkv", [2, 128, 4096], F32, kind="ExternalInput").ap()
    w_f3a = dt("w_f3a", [11, 128, 4096], F32, kind="ExternalInput").ap()
    w_f3b = dt("w_f3b", [4, 128, NF * 256], F32, kind="ExternalInput").ap()
    wq_d = dt("wq", [4, 128, 4096], F32, kind="ExternalInput").ap()
    wo1_d = dt("wo1", [4, 64, 16 * 256], F32, kind="ExternalInput").ap()
    w_f4a = dt("w_f4a", [11, 128, 4096], F32, kind="ExternalInput").ap()
    w_f4b = dt("w_f4b", [4, 128, NF * 256], F32, kind="ExternalInput").ap()
    csq_d = dt("csq", [2, 128, TQ], F32, kind="ExternalInput").ap()
    cso_d = dt("cso", [2, 128, T], F32, kind="ExternalInput").ap()
    sinks_d = dt("sinks", [1, 16], F32, kind="ExternalInput").ap()
    mk_d = dt("mk", [3, 128, 512], BF16, kind="ExternalInput").ap()
    cst_d = dt("cst", [6, 128, 512], BF16, kind="ExternalInput").ap()
    outT = dt("outT", [D, T], F32, kind="ExternalOutput").ap()
    h1_all = dt("h1_all", [D, VS], F32).ap()
    q_all = dt("q_all", [D, VS], BF16).ap()
    k_all = dt("k_all", [D, VS], BF16).ap()
    v_all = dt("v_all", [VS, D], BF16).ap()
    oq = dt("oq", [D, TQ], BF16).ap()
    h3q = dt("h3q", [D, TQ], F32).ap()
    kshq = dt("kshq", [256, TQ], BF16).ap()
    vshq = dt("vshq", [TQ, 256], BF16).ap()

    for ses in _stage("1"):
        P = Prog(ctx, ses)
        K = Tok(P, 1024)
        qkst = [P.sb(f"qkst{i}", [128, 512], BF16) for i in range(2)]
        vst = [P.sb(f"vst{i}", [128, D], BF16) for i in range(2)]
        K.load_gains(gains_d, 2)
        K.add_hbuf()
        st = {"qk": 0, "v": 0}
        NCH = VS // 1024
        K.use_hbuf(0)
        K.load_h([(xv[:, 0:1024], 0)])
        for ch in range(NCH):
            cols = slice(ch * 1024, (ch + 1) * 1024)
            if ch + 1 < NCH:
                K.use_hbuf(ch + 1)
                K.load_h([(xv[:, (ch + 1) * 1024:(ch + 2) * 1024], 0)])
            K.use_hbuf(ch)
            K.ffn(0, w_f1a, w_f1b)
            isq = (ch % 2 == 1)
            if isq:
                K.store_h([(h1_all[:, cols], 0)])
            emit_qkv(P, K, 1, wqkv_d, q_all[:, cols], k_all[:, cols], v_all[ch * 1024:(ch + 1) * 1024, :],
                     qkst, vst, st, want_q=isq)
        P.emit()

    for ses in _stage("2a"):
        P = Prog(ctx, ses)
        S_ = SBState(P)
        S_.load_consts(P, cst_d)
        NB = VS // 128
        k_sb = [P.sb(f"k_sb{i}", [128, VS], BF16) for i in range(2)]
        q_sb = [P.sb(f"q_sb{i}", [128, 2, TQ], BF16) for i in range(2)]
        for i in range(2):
            P.memset("pool", q_sb[i][:], 0.0, [("q", i)])
        v_sb = [P.sb(f"v_sb{i}", [128, NB, 128], BF16) for i in range(2)]
        o_sb = [P.sb(f"o_sb{i}", [128, TQ], BF16) for i in range(2)]
        qh_sb = [P.sb(f"qh_sb{i}", [128, 2, 512], BF16) for i in range(2)]
        for i in range(2):
            P.memset("pool", qh_sb[i][:], 0.0, [("q", i)])
        v_view = v_all.rearrange("(n s) f -> s n f", s=128)
        def load_pair(hp):
            sl = hp % 2
            rows = slice(hp * 128, (hp + 1) * 128)
            if DBG_LD & 1:
                P.dma("sp", k_sb[sl][:, :], k_all[rows, :], f"ldk{sl}", (), [("k", sl)], group=hp)
            for j in range(NSEG if DBG_LD & 2 else 0):
                for e in range(2):
                    P.dma("sp", q_sb[sl][e * 64:(e + 1) * 64, e, j * SEGW:(j + 1) * SEGW],
                          q_all[hp * 128 + e * 64:hp * 128 + (e + 1) * 64, seg_start(j):seg_start(j) + SEGW],
                          f"ldk{sl}", (), [("q", sl)], group=hp)
                    P.dma("sp", qh_sb[sl][e * 64:(e + 1) * 64, e, (3 - j) * 128:(4 - j) * 128],
                          q_all[hp * 128 + e * 64:hp * 128 + (e + 1) * 64, seg_start(j):seg_start(j) + 128],
                          f"ldk{sl}", (), [("q", sl)], group=hp)
            for n0 in range(0, NB if DBG_LD & 4 else 0, 8):
                P.dma("sp", v_sb[sl][:, n0:n0 + 8, :], v_view[:, n0:n0 + 8, hp * 128:(hp + 1) * 128],
                      f"ldk{sl}", (), [("v", sl)], group=hp)

        load_pair(0)
        for hp in range(DBG_NHP):
            sl = hp % 2
            rows = slice(hp * 128, (hp + 1) * 128)
            if hp + 1 < DBG_NHP:
                load_pair(hp + 1)
            items = []
            for e in range(2):
                rs = slice(e * 64, e * 64 + 64)
                kq = (("k", sl), ("q", sl), ("v", sl))
                vfn = (lambda kb, sl=sl: v_sb[sl][:, kb, :])
                for j in range(NSEG):
                    vt = 4 * j + 3
                    ocols = slice(j * SEGW + 128, (j + 1) * SEGW)
                    blks = []
                    for m, kb in enumerate(range(4 * vt + 3, -1, -1)):
                        rel = kb - 4 * vt
                        blks.append((kb, 512, 0 if m == 0 else 512, (0, 512, rel) if rel >= 0 else None))

                    def out_own(ot, bkey, sl=sl, rs=rs, j=j, e=e, ocols=ocols):
                        P.copy("dve", o_sb[sl][rs, ocols], P.banks[ot][rs, 0:512], [bkey], [("o", sl, e, j, ocols.start)])
                    if DBG_OWN:
                        items.append(dict(k=k_sb[sl][:, :], q=q_sb[sl][:, e, ocols], v=vfn, rs=rs, blocks=blks, Wmax=512,
                                          out=out_own, keys=kq))
                blks = []
                for m, kb in enumerate(range(16 * 3 + 11, -1, -1)):
                    a = sum(1 for j in range(NSEG) if 16 * j + 11 >= kb)
                    enter = any(16 * j + 11 == kb for j in range(NSEG))
                    W = 128 * a
                    Wold = (W - 128) if enter else W
                    blks.append((kb, W, Wold, (Wold, W, 0) if enter else None))

                def out_halo(ot, bkey, sl=sl, rs=rs, e=e):
                    for j in range(NSEG):
                        P.copy("dve", o_sb[sl][rs, j * SEGW:j * SEGW + 128], P.banks[ot][rs, (3 - j) * 128:(4 - j) * 128],
                               [bkey], [("o", sl, e, j, j * SEGW)])
                if DBG_HALO:
                    items.append(dict(k=k_sb[sl][:, :], q=qh_sb[sl][:, e, :], v=vfn, rs=rs, blocks=blks, Wmax=512,
                                      out=out_halo, keys=kq))
            emit_sb_sweep(P, S_, items)
            okeys = [("o", sl, e, j, c0) for e in range(2) for j in range(NSEG) for c0 in (j * SEGW, j * SEGW + 128)]
            P.dma("sp", oq[rows, :], o_sb[sl][:, :], f"sto{sl}", okeys, [])
        P.emit()

    for ses in _stage("2b"):
        P = Prog(ctx, ses)
        K = Tok(P, 2 * SEGW, TH=2 * SEGW)
        csb = [P.sb(f"csb{i}", [128, 512], F32) for i in range(2)]
        kst = [P.sb(f"kst{i}", [128, 512], BF16) for i in range(2)]
        vst = [P.sb(f"vst{i}", [128, 256], BF16) for i in range(2)]
        for w_ in (2, 3):
            P.dma("sp", K.gains[:, w_, :], gains_d[w_], "const", (), [("gain", w_)], group="c2")
        st = {"k": 0, "v": 0}
        for ch in range(2):
            segs = (2 * ch, 2 * ch + 1)
            K.load_h([(h1_all[:, seg_start(j):seg_start(j) + SEGW], (j % 2) * SEGW) for j in segs])
            qc = slice(ch * 2 * SEGW, (ch + 1) * 2 * SEGW)
            emit_oproj_T(P, K, [(oq[:, qc], 0)], wo0_d)
            K.ffn(2, w_f2a, w_f2b)
            K.store_h([(h3q[:, qc], 0)])
            emit_kv(P, K, 3, wkv_d, [csq_d[0][:, qc], csq_d[1][:, qc]], csb, kst, vst,
                    kshq[:, qc], vshq[ch * 2 * SEGW:(ch + 1) * 2 * SEGW, :], st)
        P.emit()

    for ses in _stage("3"):
        P = Prog(ctx, ses)
        K = Tok(P, T)
        csb = [P.sb(f"csb{i}", [128, 512], F32) for i in range(2)]
        mk_sb = P.sb("mk_sb", [128, 3, 512], BF16)
        sk_sb = P.sb("sk_sb", [65, 16], F32)
        w_sb = [P.sb(f"w_sb{i}", [128, 512], BF16) for i in range(2)]
        dn_sb = P.sb("dn_sb", [65, 512], F32)
        rd_sb = P.sb("rd_sb", [65, 512], F32)
        bc_sb = P.sb("bc_sb", [64, 512], F32)
        for w_ in range(4, 8):
            P.dma("sp", K.gains[:, w_, :], gains_d[w_], "const", (), [("gain", w_)], group="c3")
        for i in range(3):
            P.dma("sp", mk_sb[:, i, :], mk_d[i], "const", (), [("mk", i)], group="c3")
        P.dma("sp", sk_sb[64:65, :], sinks_d[:, :], "const", (), ["sk"], group="c3")
        K.load_h([(h3q[:, j * SEGW + 128:(j + 1) * SEGW], j * 512) for j in range(NSEG)])
        K.ffn(4, w_f3a, w_f3b)
        P.barrier()
        emit_swa_layer(P, K, 5, wq_d, wo1_d, cso_d, csb, kshq, vshq, TQ // 128, [5 * j for j in range(NSEG)],
                       mk_sb, sk_sb, w_sb, dn_sb, rd_sb, bc_sb)
        P.barrier()
        K.ffn(6, w_f4a, w_f4b)
        for half in range(K.NH):
            K.rmsnorm_half(half, 7, final=True)
        K.store_h([(outT, 0)], "sto")
        P.emit()
    return nc


def tile_cols(Wm, ncols_tile=512):
    Kd, N = Wm.shape
    nt = N // ncols_tile
    x = Wm.reshape(8, 128, nt, 2, 256)
    x = x.transpose(2, 1, 3, 0, 4)
    return np.ascontiguousarray(x).reshape(nt, 128, 4096)


def tile_win(w_in):
    g = w_in[:, :DFF].reshape(8, 128, 11, 256)
    u = w_in[:, DFF:].reshape(8, 128, 11, 256)
    x = np.stack([g, u], axis=0)
    x = x.transpose(3, 2, 0, 1, 4)
    return np.ascontiguousarray(x).reshape(11, 128, 4096)


def tile_wout(w_out):
    x = w_out.reshape(NF, 128, 4, 256)
    x = x.transpose(2, 1, 0, 3)
    return np.ascontiguousarray(x).reshape(4, 128, NF * 256)


def gain_layout(g):
    return np.ascontiguousarray(g.reshape(8, 128).T)


def sb_consts():
    import ml_dtypes
    c = np.zeros((6, 128, 512), np.float32)
    s = np.arange(128)[:, None]
    t = np.arange(512)[None, :]
    for r in range(4):
        c[r] = ((r * 128 + s) < t)
    j = np.arange(128)[:, None]
    ss = np.arange(128)[None, :]
    c[4, :, :128] = -(j >= ss).astype(np.float32)
    c[5] = -1.0
    return c.astype(ml_dtypes.bfloat16)


def perm_half(Wm, nheads):
    Kd = Wm.shape[0]
    x = Wm.reshape(Kd, nheads, 2, 32)[:, :, ::-1, :]
    return np.ascontiguousarray(x).reshape(Kd, nheads * 64)


def rope_tables(pos0, Tn):
    half = 32
    inv = (10000.0 ** (-np.arange(half, dtype=np.float32) / half)).astype(np.float32)
    pos = np.arange(pos0, pos0 + Tn, dtype=np.float32)
    ang = pos[None, :] * inv[:, None]
    cos = np.cos(ang).astype(np.float32)
    sin = np.sin(ang).astype(np.float32)
    c64 = np.concatenate([cos, cos], 0)
    s64 = np.concatenate([-sin, sin], 0)
    return np.ascontiguousarray(np.stack([np.concatenate([c64, c64], 0), np.concatenate([s64, s64], 0)]))


def tile_wo8(Wo):
    x = Wo.reshape(8, 128, 4, 256).transpose(2, 1, 0, 3)
    return np.ascontiguousarray(x).reshape(4, 128, 8 * 256)


def tile_pair(Wa, Wb):
    k = Wa.shape[1] // 256
    xa = Wa.reshape(8, 128, k, 256)
    xb = Wb.reshape(8, 128, k, 256)
    x = np.stack([xa, xb], 0).transpose(3, 2, 0, 1, 4)
    return np.ascontiguousarray(x).reshape(k, 128, 4096)


def swa_masks(seq_start):
    import ml_dtypes
    s = np.arange(128)[:, None]
    t = np.tile(np.arange(128), 4)[None, :]
    prev = (s > t).astype(np.float32)
    cur = (s <= t).astype(np.float32)
    first = np.zeros_like(prev) if seq_start else prev
    return np.stack([first, prev, cur]).astype(ml_dtypes.bfloat16)


def kernel(x, ffn1_norm, ffn1_w_in, ffn1_w_out, mix_norm, ffn2_norm, ffn2_w_in, ffn2_w_out,
           sb_w_qkv, sb_w_o, kv_norm, kv_w, swa_w_q, swa_sinks, swa_w_o, final_norm):
    f = lambda a: np.asarray(a, dtype=np.float32)
    x = f(x)
    CPS = NCORES // B
    kvw = f(kv_w)
    Wk, Wv = kvw[:, :256], kvw[:, 256:]
    Wq = f(swa_w_q)[0]
    order = []
    for gp in range(2):
        for j in range(4):
            order += [4 * (2 * gp) + j, 4 * (2 * gp + 1) + j]
    Wq2 = np.ascontiguousarray(Wq.reshape(D, 16, 64)[:, order, :]).reshape(D, D)
    Wo1 = f(swa_w_o)[0]
    shared = {
        "gains": np.stack([gain_layout(g) for g in (f(ffn1_norm)[0], f(mix_norm)[0], f(ffn2_norm)[0], f(kv_norm),
                                                    f(ffn1_norm)[1], f(mix_norm)[1], f(ffn2_norm)[1], f(final_norm))]),
        "w_f1a": tile_win(f(ffn1_w_in)[0]), "w_f1b": tile_wout(f(ffn1_w_out)[0]),
        "wqkv": tile_cols(f(sb_w_qkv)[0]),
        "wo0": tile_wo8(f(sb_w_o)[0]),
        "w_f2a": tile_win(f(ffn2_w_in)[0]), "w_f2b": tile_wout(f(ffn2_w_out)[0]),
        "wkv": np.concatenate([tile_pair(Wk, perm_half(Wk, 4)), tile_pair(Wv, Wv)], 0),
        "w_f3a": tile_win(f(ffn1_w_in)[1]), "w_f3b": tile_wout(f(ffn1_w_out)[1]),
        "wq": tile_pair(Wq2, perm_half(Wq2, 16)),
        "wo1": np.ascontiguousarray(Wo1.reshape(16, 64, 4, 256).transpose(2, 1, 0, 3)).reshape(4, 64, 16 * 256),
        "w_f4a": tile_win(f(ffn2_w_in)[1]), "w_f4b": tile_wout(f(ffn2_w_out)[1]),
        "sinks": f(swa_sinks)[0][None, :],
        "cst": sb_consts(),
    }
    maps = []
    for c in range(NCORES):
        b, r = c // CPS, c % CPS
        shift = (CPS - 1 - r) * 512
        xv = np.zeros((D, VS), np.float32)
        xv[:, shift:] = x[b, :VS - shift].T
        csq = np.concatenate([rope_tables(seg_start(j) - shift, SEGW) for j in range(NSEG)], axis=2)
        cso = np.concatenate([rope_tables(seg_start(j) + 128 - shift, 512) for j in range(NSEG)], axis=2)
        maps.append(dict(shared, xv=xv, csq=np.ascontiguousarray(csq), cso=np.ascontiguousarray(cso),
                         mk=swa_masks(r == 0)))
    es = ExitStack()
    nc = build_fused(es)
    res = run_bass_kernel_spmd(nc, maps, core_ids=list(range(NCORES)))
    es.close()
    out = np.empty((B, S, D), np.float32)
    for c in range(NCORES):
        b, r = c // CPS, c % CPS
        oT = res.results[c]["outT"]
        for j in range(NSEG):
            t0 = (4 * j + r) * 512
            out[b, t0:t0 + 512, :] = oT[:, j * 512:(j + 1) * 512].T
    return out
```
